# Optimizing a Trainium2 kernel written in Bass

```python
import math
import jax, jax.numpy as jnp
from jax import lax
import numpy as np

D_MODEL = 1024
BATCH = 16
SEQ = 4096
DEPTH = 1

GRID_W = 64
CTX_LEN = 256
N_HEADS = 8
HEAD_DIM = 64
V_DIM = 2 * HEAD_DIM
ATTN_WIDTH = N_HEADS * V_DIM
S5_WIDTH = D_MODEL // 2
S5_GROUP = 16
S5_GROUPS = S5_WIDTH // S5_GROUP
S5_STATE = 64
D_FF = ((8 * D_MODEL // 3 + 127) // 128) * 128
N_MOD = 9
ROPE_BASE = 10000.0
Q_BLOCK = 128
EPS = 1e-6
DT_MIN = 1e-3
DT_MAX = 1e-1
Q_COLS = N_HEADS * 2 * HEAD_DIM
K_COLS = N_HEADS * 2 * HEAD_DIM
V_COLS = N_HEADS * V_DIM
U_COLS = S5_WIDTH
IN_COLS = Q_COLS + K_COLS + V_COLS + U_COLS + 2 * D_MODEL
SPLIT_POINTS = (Q_COLS, Q_COLS + K_COLS, Q_COLS + K_COLS + V_COLS,
                Q_COLS + K_COLS + V_COLS + U_COLS,
                Q_COLS + K_COLS + V_COLS + U_COLS + D_MODEL)

kernel_name = "hybrid_diffattn_s5_macaron_block"


def rms_norm(x, g):
    xf = x.astype(jnp.float32)
    y = xf * lax.rsqrt(jnp.mean(xf * xf, axis=-1, keepdims=True) + EPS)
    return (y * g.astype(jnp.float32)).astype(x.dtype)


def modulate(x, g, shift, scale):
    return rms_norm(x, g) * (1.0 + scale) + shift


def swiglu(h, w13, w2):
    a, b = jnp.split(h @ w13, 2, axis=-1)
    return (jax.nn.silu(a) * b) @ w2


def split_proj(p):
    B, L, _ = p.shape
    q, k, v, u, ga, gs = jnp.split(p, SPLIT_POINTS, axis=-1)
    q = q.reshape(B, L, N_HEADS, 2, HEAD_DIM)
    k = k.reshape(B, L, N_HEADS, 2, HEAD_DIM)
    v = v.reshape(B, L, N_HEADS, V_DIM)
    return q, k, v, u, ga, gs


def axial_rope_tables(rows):
    n_freq = HEAD_DIM // 4
    inv = ROPE_BASE ** (-jnp.arange(n_freq, dtype=jnp.float32) / n_freq)
    row = jnp.repeat(jnp.arange(rows, dtype=jnp.float32), GRID_W)
    col = jnp.tile(jnp.arange(GRID_W, dtype=jnp.float32), rows)
    ang = jnp.concatenate([row[:, None] * inv, col[:, None] * inv], axis=-1)
    return jnp.cos(ang), jnp.sin(ang)


def apply_rope(t, cos, sin):
    half = HEAD_DIM // 2
    tf = t.astype(jnp.float32)
    t1, t2 = tf[..., :half], tf[..., half:]
    cs = cos[None, :, None, None, :]
    sn = sin[None, :, None, None, :]
    return jnp.concatenate([t1 * cs - t2 * sn, t2 * cs + t1 * sn], axis=-1).astype(t.dtype)


def diff_lambda(lq1, lk1, lq2, lk2, lam_init):
    f = lambda a: a.astype(jnp.float32)
    return jnp.exp(jnp.sum(f(lq1) * f(lk1))) - jnp.exp(jnp.sum(f(lq2) * f(lk2))) + lam_init


def diff_attend(q, k, v, lam):
    s = jnp.einsum('bqhid,bkhid->bhiqk', q, k,
                   preferred_element_type=jnp.float32) / math.sqrt(HEAD_DIM)
    p = jax.nn.softmax(s, axis=-1)
    a = p[:, :, 0] - lam * p[:, :, 1]
    return jnp.einsum('bhqk,bkhe->bqhe', a.astype(v.dtype), v)


def diff_attend_blocked(q, k, v, lam):
    B, L = q.shape[:2]
    nb = L // Q_BLOCK
    qb = jnp.moveaxis(q.reshape(B, nb, Q_BLOCK, N_HEADS, 2, HEAD_DIM), 1, 0)
    ob = lax.map(lambda qq: diff_attend(qq, k, v, lam), qb)
    return jnp.moveaxis(ob, 0, 1).reshape(B, L, N_HEADS, V_DIM)


def s5_discretize(lam_re, lam_im, log_dt, b_re, b_im):
    dt = jnp.exp(log_dt)[:, None]
    mag = jnp.exp(lam_re * dt)
    ar = mag * jnp.cos(lam_im * dt)
    ai = mag * jnp.sin(lam_im * dt)
    den = lam_re * lam_re + lam_im * lam_im
    nr = ar - 1.0
    fr = (nr * lam_re + ai * lam_im) / den
    fi = (ai * lam_re - nr * lam_im) / den
    bbr = fr[..., None] * b_re - fi[..., None] * b_im
    bbi = fr[..., None] * b_im + fi[..., None] * b_re
    return ar, ai, bbr, bbi


def _lin_combine(left, right):
    ar_l, ai_l, br_l, bi_l = left
    ar_r, ai_r, br_r, bi_r = right
    ar = ar_r * ar_l - ai_r * ai_l
    ai = ar_r * ai_l + ai_r * ar_l
    br = ar_r * br_l - ai_r * bi_l + br_r
    bi = ar_r * bi_l + ai_r * br_l + bi_r
    return ar, ai, br, bi


def s5_scan(u, ar, ai, bbr, bbi, h0r, h0i):
    br = jnp.einsum('blgc,gpc->blgp', u, bbr)
    bi = jnp.einsum('blgc,gpc->blgp', u, bbi)
    br = br.at[:, 0].add(ar * h0r - ai * h0i)
    bi = bi.at[:, 0].add(ar * h0i + ai * h0r)
    L = u.shape[1]
    arL = jnp.broadcast_to(ar, (1, L) + ar.shape)
    aiL = jnp.broadcast_to(ai, (1, L) + ai.shape)
    _, _, hr, hi = lax.associative_scan(_lin_combine, (arL, aiL, br, bi), axis=1)
    return hr, hi


def s5_readout(hr, hi, c_re, c_im):
    return (jnp.einsum('gcp,blgp->blgc', c_re, hr)
            - jnp.einsum('gcp,blgp->blgc', c_im, hi))


def s5_output(y, u, d_skip, w_glu, b_glu, dtype):
    B, L = y.shape[:2]
    y = y.reshape(B, L, S5_WIDTH) + d_skip.astype(jnp.float32) * u.reshape(B, L, S5_WIDTH)
    y = jax.nn.gelu(y.astype(dtype))
    return y * jax.nn.sigmoid(y @ w_glu + b_glu)


def s5_branch(u_x, u_c, lam_re, lam_im, log_dt, b_re, b_im, c_re, c_im,
              d_skip, w_glu, b_glu, with_ctx):
    f32 = jnp.float32
    B = u_x.shape[0]
    gx = u_x.astype(f32).reshape(B, u_x.shape[1], S5_GROUPS, S5_GROUP)
    gc = u_c.astype(f32).reshape(B, u_c.shape[1], S5_GROUPS, S5_GROUP)
    zeros = jnp.zeros((B, S5_GROUPS, S5_STATE), f32)
    y_x = 0.0
    y_c = 0.0
    for d in range(2):
        rev = d == 1
        ar, ai, bbr, bbi = s5_discretize(lam_re[d].astype(f32), lam_im[d].astype(f32),
                                         log_dt[d].astype(f32), b_re[d].astype(f32),
                                         b_im[d].astype(f32))
        ux_d = jnp.flip(gx, axis=1) if rev else gx
        uc_d = jnp.flip(gc, axis=1) if rev else gc
        hcr, hci = s5_scan(uc_d, ar, ai, bbr, bbi, zeros, zeros)
        hxr, hxi = s5_scan(ux_d, ar, ai, bbr, bbi, hcr[:, -1], hci[:, -1])
        cr, ci = c_re[d].astype(f32), c_im[d].astype(f32)
        yx = s5_readout(hxr, hxi, cr, ci)
        y_x = y_x + (jnp.flip(yx, axis=1) if rev else yx)
        if with_ctx:
            yc = s5_readout(hcr, hci, cr, ci)
            y_c = y_c + (jnp.flip(yc, axis=1) if rev else yc)
    out_x = s5_output(y_x, gx, d_skip, w_glu, b_glu, u_x.dtype)
    out_c = s5_output(y_c, gc, d_skip, w_glu, b_glu, u_c.dtype) if with_ctx else None
    return out_x, out_c


def merge_branches(o_attn, o_s5, g_attn, g_s5, w_pa, w_ps, w_out):
    m = jax.nn.sigmoid(g_attn) * (o_attn @ w_pa) + jax.nn.sigmoid(g_s5) * (o_s5 @ w_ps)
    return m @ w_out


def setup_inputs(seed: int = 0) -> dict:
    key = jax.random.key(seed)
    ks = iter(jax.random.split(key, 48))
    f32 = jnp.float32
    nrm = lambda shape, s: s * jax.random.normal(next(ks), shape, f32)
    gain = lambda shape: 1.0 + 0.02 * jax.random.normal(next(ks), shape, f32)
    Lz = DEPTH
    s5_shape = (Lz, 2, S5_GROUPS, S5_STATE)
    return {
        "x": nrm((BATCH, SEQ, D_MODEL), 1.0),
        "c": nrm((BATCH, D_MODEL), 1.0),
        "ctx": nrm((BATCH, CTX_LEN, D_MODEL), 1.0),
        "c_ctx": nrm((D_MODEL,), 1.0),
        "w_mod": nrm((Lz, D_MODEL, N_MOD * D_MODEL), 0.5 * D_MODEL ** -0.5),
        "b_mod": nrm((Lz, N_MOD * D_MODEL), 0.02),
        "norm_ffn1": gain((Lz, D_MODEL)),
        "w13_ffn1": nrm((Lz, D_MODEL, 2 * D_FF), D_MODEL ** -0.5),
        "w2_ffn1": nrm((Lz, D_FF, D_MODEL), D_FF ** -0.5),
        "norm_mix": gain((Lz, D_MODEL)),
        "w_in": nrm((Lz, D_MODEL, IN_COLS), D_MODEL ** -0.5),
        "q_norm": gain((Lz, HEAD_DIM)),
        "k_norm": gain((Lz, HEAD_DIM)),
        "lam_q1": nrm((Lz, HEAD_DIM), 0.1),
        "lam_k1": nrm((Lz, HEAD_DIM), 0.1),
        "lam_q2": nrm((Lz, HEAD_DIM), 0.1),
        "lam_k2": nrm((Lz, HEAD_DIM), 0.1),
        "subln": gain((Lz, V_DIM)),
        "s5_lam_re": -0.5 + nrm(s5_shape, 0.01),
        "s5_lam_im": jnp.pi * jnp.arange(S5_STATE, dtype=f32) + nrm(s5_shape, 0.01),
        "s5_log_dt": jax.random.uniform(next(ks), (Lz, 2, S5_GROUPS), f32,
                                        minval=math.log(DT_MIN), maxval=math.log(DT_MAX)),
        "s5_b_re": nrm((Lz, 2, S5_GROUPS, S5_STATE, S5_GROUP), (2 * S5_GROUP) ** -0.5),
        "s5_b_im": nrm((Lz, 2, S5_GROUPS, S5_STATE, S5_GROUP), (2 * S5_GROUP) ** -0.5),
        "s5_c_re": nrm((Lz, 2, S5_GROUPS, S5_GROUP, S5_STATE), (2 * S5_STATE) ** -0.5),
        "s5_c_im": nrm((Lz, 2, S5_GROUPS, S5_GROUP, S5_STATE), (2 * S5_STATE) ** -0.5),
        "s5_d": nrm((Lz, S5_WIDTH), 0.5),
        "w_glu": nrm((Lz, S5_WIDTH, S5_WIDTH), S5_WIDTH ** -0.5),
        "b_glu": nrm((Lz, S5_WIDTH), 0.02),
        "w_pa": nrm((Lz, ATTN_WIDTH, D_MODEL), ATTN_WIDTH ** -0.5),
        "w_ps": nrm((Lz, S5_WIDTH, D_MODEL), S5_WIDTH ** -0.5),
        "w_out": nrm((Lz, D_MODEL, D_MODEL), D_MODEL ** -0.5),
        "norm_ffn2": gain((Lz, D_MODEL)),
        "w13_ffn2": nrm((Lz, D_MODEL, 2 * D_FF), D_MODEL ** -0.5),
        "w2_ffn2": nrm((Lz, D_FF, D_MODEL), D_FF ** -0.5),
    }


def reference(x, c, ctx, c_ctx, w_mod, b_mod, norm_ffn1, w13_ffn1, w2_ffn1, norm_mix,
              w_in, q_norm, k_norm, lam_q1, lam_k1, lam_q2, lam_k2, subln,
              s5_lam_re, s5_lam_im, s5_log_dt, s5_b_re, s5_b_im, s5_c_re, s5_c_im,
              s5_d, w_glu, b_glu, w_pa, w_ps, w_out, norm_ffn2, w13_ffn2, w2_ffn2):
    B, L, _ = x.shape
    rows = L // GRID_W
    cos, sin = axial_rope_tables(rows)
    Lc = ctx.shape[1]
    for l in range(DEPTH):
        last = l == DEPTH - 1
        lam_init = 0.8 - 0.6 * math.exp(-0.3 * l)
        mod_x = jnp.split((jax.nn.silu(c) @ w_mod[l] + b_mod[l])[:, None, :], N_MOD, axis=-1)
        mod_c = jnp.split((jax.nn.silu(c_ctx) @ w_mod[l] + b_mod[l])[None, None, :], N_MOD, axis=-1)
        shx1, scx1, gx1, shx2, scx2, gx2, shx3, scx3, gx3 = mod_x
        shc1, scc1, gc1, shc2, scc2, gc2, shc3, scc3, gc3 = mod_c

        x = x + 0.5 * gx1 * swiglu(modulate(x, norm_ffn1[l], shx1, scx1), w13_ffn1[l], w2_ffn1[l])
        ctx = ctx + 0.5 * gc1 * swiglu(modulate(ctx, norm_ffn1[l], shc1, scc1), w13_ffn1[l], w2_ffn1[l])

        qx, kx, vx, ux, gax, gsx = split_proj(modulate(x, norm_mix[l], shx2, scx2) @ w_in[l])
        qc, kc, vc, uc, gac, gsc = split_proj(modulate(ctx, norm_mix[l], shc2, scc2) @ w_in[l])

        qx = apply_rope(rms_norm(qx, q_norm[l]), cos, sin)
        kx = apply_rope(rms_norm(kx, k_norm[l]), cos, sin)
        qc = rms_norm(qc, q_norm[l])
        kc = rms_norm(kc, k_norm[l])
        lam = diff_lambda(lam_q1[l], lam_k1[l], lam_q2[l], lam_k2[l], lam_init)
        k_all = jnp.concatenate([kc, kx], axis=1)
        v_all = jnp.concatenate([vc, vx], axis=1)
        ox = diff_attend_blocked(qx, k_all, v_all, lam)
        ox = (rms_norm(ox, subln[l]) * (1.0 - lam_init)).reshape(B, L, ATTN_WIDTH)

        sx, s5c = s5_branch(ux, uc, s5_lam_re[l], s5_lam_im[l], s5_log_dt[l], s5_b_re[l],
                            s5_b_im[l], s5_c_re[l], s5_c_im[l], s5_d[l], w_glu[l], b_glu[l],
                            with_ctx=not last)
        x_mix = merge_branches(ox, sx, gax, gsx, w_pa[l], w_ps[l], w_out[l])
        if not last:
            oc = diff_attend(qc, kc, vc, lam)
            oc = (rms_norm(oc, subln[l]) * (1.0 - lam_init)).reshape(B, Lc, ATTN_WIDTH)
            ctx = ctx + gc2 * merge_branches(oc, s5c, gac, gsc, w_pa[l], w_ps[l], w_out[l])
        x = x + gx2 * x_mix

        x = x + 0.5 * gx3 * swiglu(modulate(x, norm_ffn2[l], shx3, scx3), w13_ffn2[l], w2_ffn2[l])
        if not last:
            ctx = ctx + 0.5 * gc3 * swiglu(modulate(ctx, norm_ffn2[l], shc3, scc3), w13_ffn2[l], w2_ffn2[l])
    return x
```

```python
import math
import numpy as np
import concourse.bass as bass
import concourse.mybir as mybir
from concourse.bass_utils import run_bass_kernel_spmd
from contextlib import ExitStack

F32 = mybir.dt.float32
BF16 = mybir.dt.bfloat16
AF = mybir.ActivationFunctionType
ALU = mybir.AluOpType
AX = mybir.AxisListType

PE, ACT, DVE, POOL, SP = "pe", "act", "dve", "pool", "sp"
ENGS = (PE, ACT, DVE, POOL, SP)
NDMA = 12
NDMA_P = 4

D = 1024
L = 4096
LC = 256
LK = L + LC
DFF = 2816
NB = 2
EPS = 1e-6
LAM_INIT = 0.2


class Buf:
    __slots__ = ("name", "w", "r", "excl")

    def __init__(self, name=""):
        self.name = name
        self.w = None
        self.r = {}
        self.excl = False


class Op:
    __slots__ = ("eng", "idx", "fn", "deps", "target", "semval", "dma", "dsem", "dval")

    def __init__(self, eng, idx, fn, dma):
        self.eng, self.idx, self.fn, self.dma = eng, idx, fn, dma
        self.deps = []
        self.target = False
        self.semval = None
        self.dsem = None
        self.dval = None


class Prog:
    def __init__(self, nc):
        self.nc = nc
        self.q = {e: [] for e in ENGS}
        self.ndma = 0
        self.ndma_p = 0
        self.dma_last = [None] * (NDMA + NDMA_P)
        self.done = {e: 0 for e in ENGS}
        self.cnt = {e: 0 for e in ENGS}
        self.known = {e: {} for e in ENGS}

    def op(self, eng, fn, reads=(), writes=(), dma=False):
        o = Op(eng, len(self.q[eng]), fn, dma)
        deps = {}

        def add(p):
            if p is None:
                return
            if (not p.dma) and p.idx < self.done[p.eng]:
                return
            if (not p.dma) and p.eng == eng and not dma and eng == PE:
                return
            deps[id(p)] = p

        for b in reads:
            if b.excl:
                for p in b.r.values():
                    if p.eng != eng:
                        add(p)
            p = b.w
            if p is not None:
                if (not p.dma) and p.idx < self.done[p.eng]:
                    pass
                elif (not p.dma) and p.eng == eng and not dma:
                    deps[id(p)] = p
                else:
                    add(p)
        for b in writes:
            add(b.w)
            for p in b.r.values():
                add(p)
        if dma:
            if eng == POOL:
                slot = NDMA + self.ndma_p % NDMA_P
                o.dval = 16 * (self.ndma_p // NDMA_P + 1)
                self.ndma_p += 1
            else:
                slot = self.ndma % NDMA
                o.dval = 16 * (self.ndma // NDMA + 1)
                self.ndma += 1
            prev = self.dma_last[slot]
            if prev is not None:
                deps[id(prev)] = prev
            self.dma_last[slot] = o
            o.dsem = slot
        o.deps = list(deps.values())
        for p in o.deps:
            p.target = True
        self.q[eng].append(o)
        for b in reads:
            key = ("d", id(o)) if dma else eng
            b.r[key] = o
        for b in writes:
            b.w = o
            b.r = {}
        return o

    def barrier(self):
        lasts = []
        for e in ENGS:
            for o in reversed(self.q[e][self.done[e]:]):
                if o.fn is not None and not o.dma:
                    lasts.append(o)
                    break
        dmas = [d for d in self.dma_last if d is not None]
        for e in ENGS:
            o = Op(e, len(self.q[e]), None, False)
            o.deps = [p for p in lasts if p.eng != e] + dmas
            for p in o.deps:
                p.target = True
            self.q[e].append(o)

    def emit(self, block, sems, dsems):
        self.barrier()
        for e in ENGS:
            for o in self.q[e][self.done[e]:]:
                if o.target and not o.dma:
                    self.cnt[e] += 1
                    o.semval = self.cnt[e]
        prog = self

        def run(e, engine):
            known = prog.known[e]
            ops = prog.q[e][prog.done[e]:]
            prog.done[e] = len(prog.q[e])
            for o in ops:
                need = {}
                for p in o.deps:
                    if p.dma:
                        k, v = ("d", p.dsem), p.dval
                    else:
                        k, v = p.eng, p.semval
                    if need.get(k, 0) < v:
                        need[k] = v
                for k, v in need.items():
                    if known.get(k, 0) >= v:
                        continue
                    known[k] = v
                    s = dsems[k[1]] if isinstance(k, tuple) else sems[k]
                    engine.wait_ge(s, v)
                if o.fn is None:
                    continue
                ins = o.fn(engine)
                if o.dma:
                    ins.then_inc(dsems[o.dsem], 16)
                elif o.target:
                    ins.then_inc(sems[e], 1)

        @block.tensor
        def _(eng):
            run(PE, eng)

        @block.scalar
        def _(eng):
            run(ACT, eng)

        @block.vector
        def _(eng):
            run(DVE, eng)

        @block.gpsimd
        def _(eng):
            run(POOL, eng)

        @block.sync
        def _(eng):
            run(SP, eng)


class Tl:
    __slots__ = ("t", "b")

    def __init__(self, t, name=""):
        self.t = t
        self.b = Buf(name)


class Ctx:
    pass


class Ops:
    def __init__(self, P):
        self.P = P

    def dma(self, out, in_, R=(), W=(), eng=SP):
        return self.P.op(eng, lambda e: e.dma_start(out=out, in_=in_), R, W, dma=True)

    def mm(self, out, lhsT, rhs, start, stop, R=(), W=()):
        return self.P.op(PE, lambda e: e.matmul(out, lhsT, rhs, start=start, stop=stop), R, W)

    def tr(self, out, in_, ident, R=(), W=()):
        return self.P.op(PE, lambda e: e.transpose(out=out, in_=in_, identity=ident), R, W)

    def act(self, out, in_, func, R=(), W=(), bias=None, scale=None, accum_out=None):
        kw = {}
        if bias is not None:
            kw["bias"] = bias
        if scale is not None:
            kw["scale"] = scale
        if accum_out is not None:
            kw["accum_out"] = accum_out
        return self.P.op(ACT, lambda e: e.activation(out=out, in_=in_, func=func, **kw), R, W)

    def tt(self, eng, out, in0, in1, op, R=(), W=()):
        return self.P.op(eng, lambda e: e.tensor_tensor(out=out, in0=in0, in1=in1, op=op), R, W)

    def ts(self, eng, out, in0, s1, op0, s2=None, op1=None, R=(), W=()):
        if op1 is None:
            return self.P.op(eng, lambda e: e.tensor_scalar(out=out, in0=in0, scalar1=s1, scalar2=None, op0=op0), R, W)
        return self.P.op(eng, lambda e: e.tensor_scalar(out=out, in0=in0, scalar1=s1, scalar2=s2, op0=op0, op1=op1), R, W)

    def stt(self, out, in0, scalar, in1, op0, op1, R=(), W=()):
        return self.P.op(DVE, lambda e: e.scalar_tensor_tensor(out=out, in0=in0, scalar=scalar, in1=in1, op0=op0, op1=op1), R, W)

    def red(self, out, in_, op, R=(), W=()):
        return self.P.op(DVE, lambda e: e.tensor_reduce(out=out, in_=in_, axis=AX.X, op=op), R, W)

    def recip(self, out, in_, R=(), W=()):
        return self.P.op(DVE, lambda e: e.reciprocal(out=out, in_=in_), R, W)

    def copy(self, eng, out, in_, R=(), W=()):
        return self.P.op(eng, lambda e: e.tensor_copy(out=out, in_=in_), R, W)

    def memset(self, eng, out, val, R=(), W=()):
        return self.P.op(eng, lambda e: e.memset(out, val), R, W)


def mk(K, es, name, shape, dtype=F32):
    t = es.enter_context(K.nc.sbuf_tensor(name, list(shape), dtype))
    return Tl(t, name)


def end_phase(K):
    with K.nc.Block() as block:
        K.P.emit(block, K.sems, K.dsems)


def phase_mod(K):
    nc, O = K.nc, K.O
    with ExitStack() as es:
        ct = mk(K, es, "ct", [128, 8, 3])
        cs = mk(K, es, "cs", [128, 8, 3], BF16)
        wm = [mk(K, es, "wm%d" % i, [128, 8, 512], BF16) for i in range(2)]
        bm = mk(K, es, "bm", [3, 9 * D])
        mo = [mk(K, es, "mo%d" % i, [3, 512]) for i in range(2)]
        O.dma(ct.t[:], K.cT.ap(), W=[ct.b])
        O.dma(bm.t[:], K.b_mod.ap().partition_broadcast(3), W=[bm.b])
        O.act(cs.t[:], ct.t[:], AF.Silu, R=[ct.b], W=[cs.b])
        wv = K.w_mod.ap().rearrange("(k p) c -> p k c", p=128)
        for cb in range(18):
            w = wm[cb % 2]
            ps = K.ps[cb % 2]
            m = mo[cb % 2]
            O.dma(w.t[:], wv[:, :, cb * 512:(cb + 1) * 512], W=[w.b], eng=POOL)
            for k in range(8):
                O.mm(ps.t[0:3, :], cs.t[:, k, :], w.t[:, k, :], k == 0, k == 7, R=[cs.b, w.b], W=[ps.b])
            O.tt(DVE, m.t[:], ps.t[0:3, :], bm.t[:, cb * 512:(cb + 1) * 512], ALU.add, R=[ps.b, bm.b], W=[m.b])
            O.dma(K.MOD.ap()[:, cb * 512:(cb + 1) * 512], m.t[:], R=[m.b], W=[K.bMOD])
        end_phase(K)


def prep_tile(K, O, src_ap, xin, jk, ss, rstd, G, SH, xm, ident, pt, xT, t, src_bufs=()):
    O.dma(xin.t[:], src_ap, R=list(src_bufs), W=[xin.b])
    O.act(jk.t[:], xin.t[:], AF.Square, R=[xin.b], W=[jk.b, ss.b], accum_out=ss.t[:])
    O.ts(DVE, rstd.t[:], ss.t[:], 1.0 / D, ALU.mult, EPS, ALU.add, R=[ss.b], W=[rstd.b])
    O.act(rstd.t[:], rstd.t[:], AF.Sqrt, R=[rstd.b], W=[rstd.b])
    O.recip(rstd.t[:], rstd.t[:], R=[rstd.b], W=[rstd.b])
    O.stt(xm.t[:], xin.t[:], rstd.t[:, 0:1], G.t[:], ALU.mult, ALU.mult, R=[xin.b, rstd.b, G.b], W=[xm.b])
    O.tt(POOL, xm.t[:], xm.t[:], SH.t[:], ALU.add, R=[xm.b, SH.b], W=[xm.b])
    for k in range(8):
        O.tr(pt[k // 4].t[:, (k % 4) * 128:(k % 4 + 1) * 128], xm.t[:, k * 128:(k + 1) * 128], ident.t[:],
             R=[xm.b, ident.b], W=[pt[k // 4].b])
    for h in range(2):
        O.act(xT.t[:, 4 * h:4 * h + 4, t * 128:(t + 1) * 128],
              pt[h].t[:, :].rearrange("p (k t) -> p k t", k=4), AF.Copy, R=[pt[h].b], W=[xT.b])


def load_mod(K, O, ms, mrow, SC, SH, GT, NG):
    m = K.MOD.ap()
    O.dma(SH.t[:], m[ms:ms + 1, mrow * D:(mrow + 1) * D].partition_broadcast(128), R=[K.bMOD], W=[SH.b])
    O.dma(SC.t[:], m[ms:ms + 1, (mrow + 1) * D:(mrow + 2) * D].partition_broadcast(128), R=[K.bMOD], W=[SC.b])
    if isinstance(NG, tuple):
        ngt, norm_d = NG
        O.dma(ngt.t[:], norm_d.ap().partition_broadcast(128), W=[ngt.b])
        NG = ngt
    O.stt(SC.t[:], SC.t[:], 1.0, NG.t[:], ALU.add, ALU.mult, R=[SC.b, NG.b], W=[SC.b])
    if GT is not None:
        O.dma(GT.t[:], m[ms:ms + 1, (mrow + 2) * D:(mrow + 3) * D].partition_broadcast(128), R=[K.bMOD], W=[GT.b])


def phase_ffn(K, tag, w13_d, w2_d, norm_d, mrow, blocks, src_buf, dst_buf):
    nc, O = K.nc, K.O
    with ExitStack() as es:
        w13 = mk(K, es, "w13" + tag, [128, 8, 2 * DFF], BF16)
        w2 = mk(K, es, "w2" + tag, [128, 22, D], BF16)
        SC = mk(K, es, "SC" + tag, [128, D])
        SH = mk(K, es, "SH" + tag, [128, D])
        GT = mk(K, es, "GT" + tag, [128, D])
        xin = [mk(K, es, "xin0" + tag, [128, D])] * 2
        xr = [mk(K, es, "xr%d" % i + tag, [128, D]) for i in range(2)]
        xm = mk(K, es, "xm" + tag, [128, D])
        tmpy = xm
        jk = mk(K, es, "jk" + tag, [128, D], BF16)
        xTs = [mk(K, es, "xT%d" % i + tag, [128, 8, 512], BF16) for i in range(2)]
        gT = mk(K, es, "gT" + tag, [128, 22, 512], BF16)
        sa = [mk(K, es, "sa0" + tag, [128, 512])] * 2
        ss = [mk(K, es, "ss%d" % i + tag, [128, 1]) for i in range(2)]
        rstd = [mk(K, es, "rstd%d" % i + tag, [128, 1]) for i in range(2)]
        ident = K.ident
        w13v = w13_d.ap().rearrange("(k p) c -> p k c", p=128)
        for k in range(8):
            for hh in range(2):
                O.dma(w13.t[:, k, hh * DFF:(hh + 1) * DFF], w13v[:, k, hh * DFF:(hh + 1) * DFF], W=[w13.b], eng=POOL)
        w2v = w2_d.ap().rearrange("(j p) d -> p j d", p=128)
        for j0 in range(0, 22, 2):
            O.dma(w2.t[:, j0:j0 + 2, :], w2v[:, j0:j0 + 2, :], W=[w2.b], eng=POOL)
        NG = (GT, norm_d)
        nt = [0]

        def prep(bi, t):
            ms, tiles = blocks[bi]
            src, dst = tiles[t]
            r = nt[0] % 2
            nt[0] += 1
            prep_tile(K, O, src, xin[r], jk, ss[r], rstd[r], SC, SH, xm, ident, K.ps[0:2], xTs[bi % 2], t,
                      src_bufs=[src_buf])

        cur_ms = None
        for bi, (ms, tiles) in enumerate(blocks):
            xT = xTs[bi % 2]
            same_next = bi + 1 < len(blocks) and blocks[bi + 1][0] == ms
            if ms != cur_ms:
                load_mod(K, O, ms, mrow, SC, SH, GT, NG)
                O.ts(DVE, GT.t[:], GT.t[:], 0.5, ALU.mult, R=[GT.b], W=[GT.b])
                cur_ms = ms
                for t in range(len(tiles)):
                    prep(bi, t)
            for j in range(22):
                pa, pb = K.ps[2 + (j % 2) * 2], K.ps[3 + (j % 2) * 2]
                for k in range(8):
                    O.mm(pa.t[:], w13.t[:, k, j * 128:(j + 1) * 128], xT.t[:, k, :], k == 0, k == 7,
                         R=[w13.b, xT.b], W=[pa.b])
                for k in range(8):
                    O.mm(pb.t[:], w13.t[:, k, DFF + j * 128:DFF + (j + 1) * 128], xT.t[:, k, :], k == 0, k == 7,
                         R=[w13.b, xT.b], W=[pb.b])
                s_ = sa[j % 2]
                O.act(s_.t[:], pa.t[:], AF.Silu, R=[pa.b], W=[s_.b])
                O.tt(DVE, gT.t[:, j, :], s_.t[:], pb.t[:], ALU.mult, R=[s_.b, pb.b], W=[gT.b])
                if same_next and j in (4, 9, 14, 19):
                    prep(bi + 1, (j - 4) // 5)
            for t, (src, dst) in enumerate(tiles):
                xo = xr[t % 2]
                O.dma(xo.t[:], src, R=[src_buf], W=[xo.b])
                for h in range(2):
                    py = K.ps[6 + h]
                    for j in range(22):
                        O.mm(py.t[:], gT.t[:, j, t * 128:(t + 1) * 128], w2.t[:, j, h * 512:(h + 1) * 512],
                             j == 0, j == 21, R=[gT.b, w2.b], W=[py.b])
                    O.tt(DVE, tmpy.t[:, h * 512:(h + 1) * 512], py.t[:], GT.t[:, h * 512:(h + 1) * 512], ALU.mult,
                         R=[py.b, GT.b], W=[tmpy.b])
                O.tt(POOL, xo.t[:], xo.t[:], tmpy.t[:], ALU.add, R=[xo.b, tmpy.b], W=[xo.b])
                O.dma(dst, xo.t[:], R=[xo.b], W=[dst_buf])
        end_phase(K)


def ffn_blocks(src_x, dst_x, src_c=None, dst_c=None):
    blocks = []
    for b in range(NB):
        for blk in range(L // 512):
            tiles = []
            for t in range(4):
                r0 = blk * 512 + t * 128
                tiles.append((src_x.ap()[b, r0:r0 + 128, :], dst_x.ap()[b, r0:r0 + 128, :]))
            blocks.append((b, tiles))
    if src_c is not None:
        tiles = []
        for b in range(NB):
            for t in range(2):
                tiles.append((src_c.ap()[b, t * 128:(t + 1) * 128, :], dst_c.ap()[b, t * 128:(t + 1) * 128, :]))
        blocks.append((2, tiles))
    return blocks


def phase_inproj(K):
    nc, O = K.nc, K.O
    with ExitStack() as es:
        w = mk(K, es, "w_in_s", [128, 8, 5632], BF16)
        NG = mk(K, es, "NGi", [128, D])
        SC = mk(K, es, "SCi", [128, D])
        SH = mk(K, es, "SHi", [128, D])
        xin = [mk(K, es, "xini%d" % i, [128, D]) for i in range(2)]
        xm = mk(K, es, "xmi", [128, D])
        jk = mk(K, es, "jki", [128, D], BF16)
        xT = mk(K, es, "xTi", [128, 8, 512], BF16)
        ss = [mk(K, es, "ssi%d" % i, [128, 1]) for i in range(2)]
        rstd = [mk(K, es, "rstdi%d" % i, [128, 1]) for i in range(2)]
        gq = mk(K, es, "gq", [128, 64])
        gk = mk(K, es, "gk", [128, 64])
        RC = mk(K, es, "RC", [128, 32, 32])
        RS = mk(K, es, "RS", [128, 32, 32])
        raw = [mk(K, es, "rawqk%d" % i, [128, 2048]) for i in range(2)]
        sqb = mk(K, es, "sqb", [128, 2048])
        ss32 = mk(K, es, "ss32", [128, 32])
        mA = mk(K, es, "mA", [128, 1024])
        mB = mk(K, es, "mB", [128, 1024])
        qr = [mk(K, es, "qr%d" % i, [128, 2048]) for i in range(2)]
        QTs = mk(K, es, "QTs", [128, 8, 512], BF16)
        KTs = mk(K, es, "KTs", [128, 8, 512], BF16)
        vs = [mk(K, es, "vs%d" % i, [128, 1024], BF16) for i in range(2)]
        us = [mk(K, es, "us%d" % i, [128, 512]) for i in range(2)]
        gsb = [mk(K, es, "gsb%d" % i, [128, 512]) for i in range(2)]
        ident = K.ident
        wv = K.w_in.ap().rearrange("(k p) c -> p k c", p=128)
        for k in range(8):
            for hh in range(2):
                O.dma(w.t[:, k, hh * DFF:(hh + 1) * DFF], wv[:, k, hh * DFF:(hh + 1) * DFF], W=[w.b], eng=POOL)
        O.dma(NG.t[:], K.norm_mix.ap().partition_broadcast(128), W=[NG.b])
        O.dma(gq.t[:], K.q_norm.ap().partition_broadcast(128), W=[gq.b])
        O.dma(gk.t[:], K.k_norm.ap().partition_broadcast(128), W=[gk.b])
        O.dma(RC.t[:], K.ropec.ap().rearrange("(n p) f -> p n f", p=128), W=[RC.b])
        O.dma(RS.t[:], K.ropes.ap().rearrange("(n p) f -> p n f", p=128), W=[RS.b])
        blocks = []
        for b in range(NB):
            for blk in range(L // 512):
                tl = [(K.X1.ap()[b, blk * 512 + t * 128: blk * 512 + (t + 1) * 128, :], b, blk * 512 + t * 128) for t in range(4)]
                blocks.append((b, True, tl))
        blocks.append((2, False, [(K.C1.ap()[b, t * 128:(t + 1) * 128, :], b, t * 128) for b in range(NB) for t in range(2)]))
        cur_ms = None
        nt = 0
        ucnt = 0
        qcnt = 0
        for (ms, lat, tiles) in blocks:
            if ms != cur_ms:
                load_mod(K, O, ms, 3, SC, SH, None, NG)
                cur_ms = ms
            for t, (src, b, pos) in enumerate(tiles):
                r = nt % 2
                nt += 1
                prep_tile(K, O, src, xin[r], jk, ss[r], rstd[r], SC, SH, xm, ident, K.ps[0:2], xT, t,
                          src_bufs=[K.bX1, K.bC1])
            if lat:
                b = tiles[0][1]
                tok0 = tiles[0][2]
                for m in range(16):
                    pg = K.ps[6 + m % 2]
                    c0 = 3584 + m * 128
                    for k in range(8):
                        O.mm(pg.t[:], w.t[:, k, c0:c0 + 128], xT.t[:, k, :], k == 0, k == 7, R=[w.b, xT.b], W=[pg.b])
                    g = gsb[m % 2]
                    O.act(g.t[:], pg.t[:], AF.Sigmoid, R=[pg.b], W=[g.b])
                    dst = (K.GA if m < 8 else K.GS).ap()[b, (m % 8) * 128:(m % 8 + 1) * 128, tok0:tok0 + 512]
                    O.dma(dst, g.t[:], R=[g.b], W=[K.bG])
            def make_chain(rw, q_r, t, lat, tile_idx):
                def chain():
                    W_ = 2048 if lat else 1024
                    G_ = W_ // 64
                    g3 = lambda tl_: tl_.t[:, 0:W_].rearrange("p (g d) -> p g d", d=64)
                    O.act(sqb.t[:, 0:W_], rw.t[:, 0:W_], AF.Square, R=[rw.b], W=[sqb.b])
                    O.red(ss32.t[:, 0:G_], g3(sqb), ALU.add, R=[sqb.b], W=[ss32.b])
                    O.ts(DVE, ss32.t[:, 0:G_], ss32.t[:, 0:G_], 1.0 / 64, ALU.mult, EPS, ALU.add, R=[ss32.b], W=[ss32.b])
                    O.act(ss32.t[:, 0:G_], ss32.t[:, 0:G_], AF.Sqrt, R=[ss32.b], W=[ss32.b])
                    O.recip(ss32.t[:, 0:G_], ss32.t[:, 0:G_], R=[ss32.b], W=[ss32.b])
                    O.tt(DVE, g3(sqb), g3(rw), ss32.t[:, 0:G_].unsqueeze(2).to_broadcast([128, G_, 64]), ALU.mult,
                         R=[rw.b, ss32.b, sqb.b], W=[sqb.b])
                    gqb = gq.t[:, :].unsqueeze(1).to_broadcast([128, 16, 64])
                    gkb = gk.t[:, :].unsqueeze(1).to_broadcast([128, 16, 64])
                    if lat:
                        O.tt(DVE, g3(sqb)[:, 0:16, :], g3(sqb)[:, 0:16, :], gqb, ALU.mult, R=[sqb.b, gq.b], W=[sqb.b])
                        O.tt(DVE, g3(sqb)[:, 16:32, :], g3(sqb)[:, 16:32, :], gkb, ALU.mult, R=[sqb.b, gk.b], W=[sqb.b])
                        v4 = sqb.t[:, 0:W_].rearrange("p (g h d) -> p g h d", h=2, d=32)
                        o4 = q_r.t[:, 0:W_].rearrange("p (g h d) -> p g h d", h=2, d=32)
                        t1, t2 = v4[:, :, 0, :], v4[:, :, 1, :]
                        cosb = RC.t[:, tile_idx:tile_idx + 1, :].to_broadcast([128, G_, 32])
                        sinb = RS.t[:, tile_idx:tile_idx + 1, :].to_broadcast([128, G_, 32])
                        mAv = mA.t[:, 0:G_ * 32].rearrange("p (g d) -> p g d", d=32)
                        mBv = mB.t[:, 0:G_ * 32].rearrange("p (g d) -> p g d", d=32)
                        O.tt(DVE, mAv, t1, cosb, ALU.mult, R=[sqb.b, RC.b], W=[mA.b])
                        O.tt(DVE, mBv, t2, sinb, ALU.mult, R=[sqb.b, RS.b], W=[mB.b])
                        O.tt(DVE, o4[:, :, 0, :], mAv, mBv, ALU.subtract, R=[mA.b, mB.b], W=[q_r.b])
                        O.tt(DVE, mAv, t2, cosb, ALU.mult, R=[sqb.b, RC.b], W=[mA.b])
                        O.tt(DVE, mBv, t1, sinb, ALU.mult, R=[sqb.b, RS.b], W=[mB.b])
                        O.tt(DVE, o4[:, :, 1, :], mAv, mBv, ALU.add, R=[mA.b, mB.b, q_r.b], W=[q_r.b])
                    else:
                        O.tt(DVE, g3(q_r), g3(sqb), gkb, ALU.mult, R=[sqb.b, gk.b], W=[q_r.b])

                def post():
                    W_ = 2048 if lat else 1024
                    for un in range(W_ // 512):
                        n = un if lat else un + 2
                        pt = K.ps[un % 2]
                        for i in range(4):
                            c0 = un * 512 + i * 128
                            O.tr(pt.t[:, i * 128:(i + 1) * 128], q_r.t[:, c0:c0 + 128], ident.t[:], R=[q_r.b, ident.b], W=[pt.b])
                        dT = QTs if n < 2 else KTs
                        hh = (n % 2) * 4
                        O.act(dT.t[:, hh:hh + 4, t * 128:(t + 1) * 128], pt.t[:, :].rearrange("p (k t) -> p k t", k=4),
                              AF.Copy, R=[pt.b], W=[dT.b])
                return chain, post

            pending = None
            for t, (src, b, pos) in enumerate(tiles):
                kpos = pos + LC if lat else pos
                tile_idx = pos // 128
                rw, q_r = raw[qcnt % 2], qr[qcnt % 2]
                qcnt += 1
                units = list(range(7)) if lat else list(range(2, 7))
                for n in units:
                    pb = K.ps[2 + ucnt % 4]
                    ucnt += 1
                    for k in range(8):
                        O.mm(pb.t[:], xT.t[:, k, t * 128:(t + 1) * 128], w.t[:, k, n * 512:(n + 1) * 512], k == 0, k == 7,
                             R=[xT.b, w.b], W=[pb.b])
                    if n < 4:
                        col = (n if lat else n - 2) * 512
                        O.act(rw.t[:, col:col + 512], pb.t[:], AF.Copy, R=[pb.b], W=[rw.b])
                    elif n < 6:
                        v_ = vs[(nt + t) % 2]
                        O.copy(DVE, v_.t[:, (n - 4) * 512:(n - 3) * 512], pb.t[:], R=[pb.b], W=[v_.b])
                        if n == 5:
                            O.dma(K.V.ap()[b, kpos:kpos + 128, :], v_.t[:], R=[v_.b], W=[K.bV])
                    else:
                        u_ = us[(nt + t) % 2]
                        O.copy(DVE, u_.t[:], pb.t[:], R=[pb.b], W=[u_.b])
                        O.dma(K.U.ap()[b, kpos:kpos + 128, :], u_.t[:], R=[u_.b], W=[K.bU])
                if pending is not None:
                    pending[0]()
                    pending[1]()
                pending = make_chain(rw, q_r, t, lat, tile_idx)
            pending[0]()
            pending[1]()
            if lat:
                b = tiles[0][1]
                tok0 = tiles[0][2]
                O.dma(K.QT.ap()[b].rearrange("(h p) t -> p h t", p=128)[:, :, tok0:tok0 + 512], QTs.t[:], R=[QTs.b], W=[K.bQT])
                O.dma(K.KT.ap()[b].rearrange("(h p) t -> p h t", p=128)[:, :, LC + tok0:LC + tok0 + 512], KTs.t[:], R=[KTs.b], W=[K.bKT])
            else:
                for b in range(NB):
                    O.dma(K.KT.ap()[b].rearrange("(h p) t -> p h t", p=128)[:, :, 0:LC], KTs.t[:, :, b * 256:(b + 1) * 256],
                          R=[KTs.b], W=[K.bKT])
        end_phase(K)


def phase_attn(K):
    nc, O = K.nc, K.O
    with ExitStack() as es:
        KTh = [mk(K, es, "KTh%d" % i, [128, LK], BF16) for i in range(2)]
        Qm = [[mk(K, es, "Qm%d_%d" % (i, c), [128, L], BF16) for c in range(2)] for i in range(2)]
        for i in range(2):
            for c in range(2):
                O.memset(POOL, Qm[i][c].t[:], 0.0, W=[Qm[i][c].b])
        Vh = [mk(K, es, "Vh%d" % i, [128, 34, 128], BF16) for i in range(2)]
        Et = [mk(K, es, "Et%d" % i, [128, 512], BF16) for i in range(4)]
        ones_b = mk(K, es, "ones_b", [128, 128], BF16)
        ones_f = mk(K, es, "ones_f", [128, 128])
        lv = mk(K, es, "lv", [128, 256])
        lp = mk(K, es, "lp", [128, 2, 64])
        ls = mk(K, es, "ls", [128, 2])
        nlam = mk(K, es, "nlam", [128, 1])
        subc = mk(K, es, "subc", [128, 1])
        rc = mk(K, es, "rc", [128, 512])
        o0 = mk(K, es, "o0", [128, 512])
        o1 = mk(K, es, "o1", [128, 512])
        osq = mk(K, es, "osq", [128, 512])
        rs = mk(K, es, "rs", [128, 512])
        OTs = [mk(K, es, "OTs%d" % i, [128, 512], BF16) for i in range(2)]
        O.memset(POOL, ones_b.t[:], 1.0, W=[ones_b.b])
        O.memset(POOL, ones_f.t[:], 1.0, W=[ones_f.b])
        O.dma(lv.t[:], K.lamv.ap().partition_broadcast(128), W=[lv.b])
        O.dma(subc.t[:], K.subln.ap(), W=[subc.b])
        lvv = lv.t[:, :].rearrange("p (a b d) -> p a b d", a=2, b=2)
        O.tt(DVE, lp.t[:], lvv[:, :, 0, :], lvv[:, :, 1, :], ALU.mult, R=[lv.b], W=[lp.b])
        O.red(ls.t[:], lp.t[:], ALU.add, R=[lp.b], W=[ls.b])
        O.act(ls.t[:], ls.t[:], AF.Exp, R=[ls.b], W=[ls.b])
        O.tt(DVE, nlam.t[:], ls.t[:, 1:2], ls.t[:, 0:1], ALU.subtract, R=[ls.b], W=[nlam.b])
        O.ts(DVE, nlam.t[:], nlam.t[:], -LAM_INIT, ALU.add, R=[nlam.b], W=[nlam.b])
        O.ts(DVE, subc.t[:], subc.t[:], 1.0 - LAM_INIT, ALU.mult, R=[subc.b], W=[subc.b])
        psc = K.ps[0:3]
        pO = K.ps[3:5]
        pS = K.ps[5:7]
        pss = K.ps[7]
        hidx = 0
        for b in range(NB):
            for h in range(8):
                kt, qm, vh = KTh[hidx % 2], Qm[hidx % 2], Vh[hidx % 2]
                hidx += 1
                O.dma(kt.t[:], K.KT.ap()[b, h * 128:(h + 1) * 128, :], R=[K.bKT], W=[kt.b])
                for c in range(2):
                    O.dma(qm[c].t[c * 64:(c + 1) * 64, :], K.QT.ap()[b, h * 128 + c * 64:h * 128 + (c + 1) * 64, :],
                          R=[K.bQT], W=[qm[c].b])
                O.dma(vh.t[:], K.V.ap()[b].rearrange("(n p) e -> p n e", p=128)[:, :, h * 128:(h + 1) * 128], R=[K.bV], W=[vh.b])
                seq = [(qb, c, kc) for qb in range(8) for c in range(2) for kc in range(34)]

                def sc(i):
                    qb, c, kc = seq[i]
                    p = psc[i % 3]
                    O.mm(p.t[:], kt.t[:, kc * 128:(kc + 1) * 128],
                         qm[c].t[:, qb * 512:(qb + 1) * 512], True, True, R=[kt.b, qm[c].b], W=[p.b])

                def ex(i):
                    p = psc[i % 3]
                    e_ = Et[i % 4]
                    O.act(e_.t[:], p.t[:], AF.Exp, R=[p.b], W=[e_.b], scale=0.125)

                def av(i):
                    qb, c, kc = seq[i]
                    e_ = Et[i % 4]
                    O.mm(pO[c].t[:], vh.t[:, kc, :], e_.t[:], kc == 0, kc == 33, R=[vh.b, e_.b], W=[pO[c].b])
                    O.mm(pS[c].t[:], ones_b.t[:], e_.t[:], kc == 0, kc == 33, R=[ones_b.b, e_.b], W=[pS[c].b])

                sc(0)
                sc(1)
                for i in range(len(seq)):
                    ex(i)
                    if i + 2 < len(seq):
                        sc(i + 2)
                    av(i)
                    qb, c, kc = seq[i]
                    if c == 1 and kc == 33:
                        ot = OTs[qb % 2]
                        O.recip(rc.t[:], pS[0].t[:], R=[pS[0].b], W=[rc.b])
                        O.tt(DVE, o0.t[:], pO[0].t[:], rc.t[:], ALU.mult, R=[pO[0].b, rc.b], W=[o0.b])
                        O.recip(rc.t[:], pS[1].t[:], R=[pS[1].b], W=[rc.b])
                        O.tt(DVE, o1.t[:], pO[1].t[:], rc.t[:], ALU.mult, R=[pO[1].b, rc.b], W=[o1.b])
                        O.stt(o0.t[:], o1.t[:], nlam.t[:, 0:1], o0.t[:], ALU.mult, ALU.add, R=[o1.b, o0.b, nlam.b], W=[o0.b])
                        O.tt(POOL, osq.t[:], o0.t[:], o0.t[:], ALU.mult, R=[o0.b], W=[osq.b])
                        O.mm(pss.t[:], ones_f.t[:], osq.t[:], True, True, R=[ones_f.b, osq.b], W=[pss.b])
                        O.ts(DVE, rs.t[:], pss.t[:], 1.0 / 128, ALU.mult, EPS, ALU.add, R=[pss.b], W=[rs.b])
                        O.act(rs.t[:], rs.t[:], AF.Sqrt, R=[rs.b], W=[rs.b])
                        O.recip(rs.t[:], rs.t[:], R=[rs.b], W=[rs.b])
                        O.stt(ot.t[:], o0.t[:], subc.t[:, 0:1], rs.t[:], ALU.mult, ALU.mult, R=[o0.b, subc.b, rs.b], W=[ot.b])
                        O.dma(K.OT.ap()[b, h * 128:(h + 1) * 128, qb * 512:(qb + 1) * 512], ot.t[:], R=[ot.b], W=[K.bOT])
        end_phase(K)


TWO_PI = 2.0 * math.pi
CW1 = 6.28125
CW2 = TWO_PI - CW1
NCH = LK // 8
NCC = LC // 8


def sincos(K, O, A, S_, C_, tmps, itile, shape_ap):
    kf, r, m = tmps
    v = shape_ap
    O.ts(DVE, v(kf), v(A), 1.0 / TWO_PI, ALU.mult, R=[A.b], W=[kf.b])
    O.copy(DVE, v(itile), v(kf), R=[kf.b], W=[itile.b])
    O.copy(DVE, v(kf), v(itile), R=[itile.b], W=[kf.b])
    O.stt(v(r), v(kf), -CW1, v(A), ALU.mult, ALU.add, R=[kf.b, A.b], W=[r.b])
    O.stt(v(r), v(kf), -CW2, v(r), ALU.mult, ALU.add, R=[kf.b, r.b], W=[r.b])
    shp = list(v(r).shape)
    pib = K.picol.t[:, 0:1].to_broadcast(shp)
    npib = K.picol.t[:, 1:2].to_broadcast(shp)
    O.tt(DVE, v(m), v(r), pib, ALU.is_gt, R=[r.b, K.picol.b], W=[m.b])
    O.stt(v(r), v(m), -TWO_PI, v(r), ALU.mult, ALU.add, R=[m.b, r.b], W=[r.b])
    O.tt(DVE, v(m), v(r), npib, ALU.is_lt, R=[r.b, K.picol.b], W=[m.b])
    O.stt(v(r), v(m), TWO_PI, v(r), ALU.mult, ALU.add, R=[m.b, r.b], W=[r.b])
    O.ts(DVE, v(r), v(r), 3.141592, ALU.min, -3.141592, ALU.max, R=[r.b], W=[r.b])
    O.act(v(S_), v(r), AF.Sin, R=[r.b], W=[S_.b])
    O.ts(DVE, v(m), v(r), -1.0, ALU.mult, R=[r.b], W=[m.b])
    O.tt(DVE, v(m), v(m), v(r), ALU.max, R=[m.b, r.b], W=[m.b])
    O.ts(DVE, v(m), v(m), -1.0, ALU.mult, math.pi / 2, ALU.add, R=[m.b], W=[m.b])
    O.act(v(C_), v(m), AF.Sin, R=[m.b], W=[C_.b])


def phase_s5(K):
    nc, O = K.nc, K.O
    with ExitStack() as esP:
        Wbm = [[mk(K, esP, "Wbm%d%d" % (ri, d), [128, 32, 128], BF16) for d in range(2)] for ri in range(2)]
        Wcm = [[mk(K, esP, "Wcm%d%d" % (d, ri), [128, 32, 128], BF16) for ri in range(2)] for d in range(2)]
        M = [mk(K, esP, "M%d" % d, [128, 32, 128], BF16) for d in range(2)]
        r8 = mk(K, esP, "r8", [128, 32])
        ph8 = mk(K, esP, "ph8", [128, 32])
        nvec = mk(K, esP, "nvec", [128, NCH])
        O.dma(nvec.t[:], K.nvecd.ap(), W=[nvec.b])
        K.picol = mk(K, esP, "picol", [128, 2])
        O.memset(DVE, K.picol.t[:, 0:1], math.pi, W=[K.picol.b])
        O.memset(DVE, K.picol.t[:, 1:2], -math.pi, W=[K.picol.b])
        ident = K.ident
        with ExitStack() as es:
            sp = mk(K, es, "s5p_s", [128, 3, 32])
            Bt = mk(K, es, "s5B_s", [128, 2, 32, 16])
            Ct = mk(K, es, "s5C_s", [128, 2, 32, 16])
            CM = mk(K, es, "CM_s", [128, 2, 128])
            O.dma(sp.t[:], K.s5p.ap(), W=[sp.b])
            O.dma(Bt.t[:], K.s5B.ap(), W=[Bt.b])
            O.dma(Ct.t[:], K.s5C.ap(), W=[Ct.b])
            O.dma(CM.t[:], K.CMd.ap(), W=[CM.b])
            sm = {n: mk(K, es, "sm_" + n, [128, 32]) for n in
                  "dt x th mag s c ar ai t1 t2 den nr fr fi kf r m th8".split()}
            smi = es.enter_context(nc.sbuf_tensor("smi", [128, 32], mybir.dt.int32))
            smi = Tl(smi, "smi")
            PWR = mk(K, es, "PWR", [128, 9, 32])
            PWI = mk(K, es, "PWI", [128, 9, 32])
            NPR = mk(K, es, "NPR", [128, 9, 32])
            NPI = mk(K, es, "NPI", [128, 9, 32])
            pt1 = mk(K, es, "pt1", [128, 8, 32])
            pt2 = mk(K, es, "pt2", [128, 8, 32])
            lre, lim, ldt = sp.t[:, 0, :], sp.t[:, 1, :], sp.t[:, 2, :]
            A = lambda n: sm[n].t[:]
            Bf = lambda n: sm[n].b
            O.act(A("dt"), ldt, AF.Exp, R=[sp.b], W=[Bf("dt")])
            O.tt(DVE, A("x"), lre, A("dt"), ALU.mult, R=[sp.b, Bf("dt")], W=[Bf("x")])
            O.tt(DVE, A("th"), lim, A("dt"), ALU.mult, R=[sp.b, Bf("dt")], W=[Bf("th")])
            O.act(A("mag"), A("x"), AF.Exp, R=[Bf("x")], W=[Bf("mag")])
            O.act(r8.t[:], A("x"), AF.Exp, R=[Bf("x")], W=[r8.b], scale=8.0)
            sincos(K, O, sm["th"], sm["s"], sm["c"], (sm["kf"], sm["r"], sm["m"]), smi, lambda tl: tl.t[:])
            O.tt(DVE, A("ar"), A("mag"), A("c"), ALU.mult, R=[Bf("mag"), Bf("c")], W=[Bf("ar")])
            O.tt(DVE, A("ai"), A("mag"), A("s"), ALU.mult, R=[Bf("mag"), Bf("s")], W=[Bf("ai")])
            O.ts(DVE, A("th8"), A("th"), 8.0, ALU.mult, R=[Bf("th")], W=[Bf("th8")])
            O.ts(DVE, A("kf"), A("th8"), 1.0 / TWO_PI, ALU.mult, R=[Bf("th8")], W=[Bf("kf")])
            O.copy(DVE, smi.t[:], A("kf"), R=[Bf("kf")], W=[smi.b])
            O.copy(DVE, A("kf"), smi.t[:], R=[smi.b], W=[Bf("kf")])
            O.stt(ph8.t[:], A("kf"), -CW1, A("th8"), ALU.mult, ALU.add, R=[Bf("kf"), Bf("th8")], W=[ph8.b])
            O.stt(ph8.t[:], A("kf"), -CW2, ph8.t[:], ALU.mult, ALU.add, R=[Bf("kf"), ph8.b], W=[ph8.b])
            O.tt(DVE, A("t1"), lre, lre, ALU.mult, R=[sp.b], W=[Bf("t1")])
            O.tt(DVE, A("t2"), lim, lim, ALU.mult, R=[sp.b], W=[Bf("t2")])
            O.tt(DVE, A("den"), A("t1"), A("t2"), ALU.add, R=[Bf("t1"), Bf("t2")], W=[Bf("den")])
            O.recip(A("den"), A("den"), R=[Bf("den")], W=[Bf("den")])
            O.ts(DVE, A("nr"), A("ar"), -1.0, ALU.add, R=[Bf("ar")], W=[Bf("nr")])
            O.tt(DVE, A("t1"), A("nr"), lre, ALU.mult, R=[Bf("nr"), sp.b], W=[Bf("t1")])
            O.tt(DVE, A("t2"), A("ai"), lim, ALU.mult, R=[Bf("ai"), sp.b], W=[Bf("t2")])
            O.tt(DVE, A("fr"), A("t1"), A("t2"), ALU.add, R=[Bf("t1"), Bf("t2")], W=[Bf("fr")])
            O.tt(DVE, A("fr"), A("fr"), A("den"), ALU.mult, R=[Bf("fr"), Bf("den")], W=[Bf("fr")])
            O.tt(DVE, A("t1"), A("ai"), lre, ALU.mult, R=[Bf("ai"), sp.b], W=[Bf("t1")])
            O.tt(DVE, A("t2"), A("nr"), lim, ALU.mult, R=[Bf("nr"), sp.b], W=[Bf("t2")])
            O.tt(DVE, A("fi"), A("t1"), A("t2"), ALU.subtract, R=[Bf("t1"), Bf("t2")], W=[Bf("fi")])
            O.tt(DVE, A("fi"), A("fi"), A("den"), ALU.mult, R=[Bf("fi"), Bf("den")], W=[Bf("fi")])
            O.memset(DVE, PWR.t[:, 0, :], 1.0, W=[PWR.b])
            O.memset(DVE, PWI.t[:, 0, :], 0.0, W=[PWI.b])
            O.copy(DVE, PWR.t[:, 1, :], A("ar"), R=[Bf("ar")], W=[PWR.b])
            O.copy(DVE, PWI.t[:, 1, :], A("ai"), R=[Bf("ai")], W=[PWI.b])
            for k in range(2, 9):
                O.tt(DVE, A("t1"), PWR.t[:, k - 1, :], A("ar"), ALU.mult, R=[PWR.b, Bf("ar")], W=[Bf("t1")])
                O.tt(DVE, A("t2"), PWI.t[:, k - 1, :], A("ai"), ALU.mult, R=[PWI.b, Bf("ai")], W=[Bf("t2")])
                O.tt(DVE, PWR.t[:, k, :], A("t1"), A("t2"), ALU.subtract, R=[Bf("t1"), Bf("t2")], W=[PWR.b])
                O.tt(DVE, A("t1"), PWR.t[:, k - 1, :], A("ai"), ALU.mult, R=[PWR.b, Bf("ai")], W=[Bf("t1")])
                O.tt(DVE, A("t2"), PWI.t[:, k - 1, :], A("ar"), ALU.mult, R=[PWI.b, Bf("ar")], W=[Bf("t2")])
                O.tt(DVE, PWI.t[:, k, :], A("t1"), A("t2"), ALU.add, R=[Bf("t1"), Bf("t2")], W=[PWI.b])
            O.tt(DVE, pt1.t[:], PWR.t[:, 1:9, :], PWR.t[:, 1:9, :], ALU.mult, R=[PWR.b], W=[pt1.b])
            O.tt(DVE, pt2.t[:], PWI.t[:, 1:9, :], PWI.t[:, 1:9, :], ALU.mult, R=[PWI.b], W=[pt2.b])
            O.tt(DVE, pt1.t[:], pt1.t[:], pt2.t[:], ALU.add, R=[pt1.b, pt2.b], W=[pt1.b])
            O.recip(pt1.t[:], pt1.t[:], R=[pt1.b], W=[pt1.b])
            O.tt(DVE, NPR.t[:, 1:9, :], PWR.t[:, 1:9, :], pt1.t[:], ALU.mult, R=[PWR.b, pt1.b], W=[NPR.b])
            O.tt(DVE, pt2.t[:], PWI.t[:, 1:9, :], pt1.t[:], ALU.mult, R=[PWI.b, pt1.b], W=[pt2.b])
            O.ts(DVE, NPI.t[:, 1:9, :], pt2.t[:], -1.0, ALU.mult, R=[pt2.b], W=[NPI.b])
            PB = [mk(K, es, "PB%d" % i, [128, 32, 8]) for i in range(2)]
            PNB = [mk(K, es, "PNB%d" % i, [128, 32, 8]) for i in range(2)]
            PC = [mk(K, es, "PC%d" % i, [128, 32, 8]) for i in range(2)]
            for j in range(8):
                for (dst, srcs, k0, k1) in ((PB, (PWR, PWI), 7 - j, j), (PNB, (NPR, NPI), 1 + j, 8 - j), (PC, (PWR, PWI), j + 1, 8 - j)):
                    for ri in range(2):
                        O.copy(POOL, dst[ri].t[0:64, :, j], srcs[ri].t[0:64, k0, :], R=[srcs[ri].b], W=[dst[ri].b])
                        O.copy(POOL, dst[ri].t[64:128, :, j], srcs[ri].t[64:128, k1, :], R=[srcs[ri].b], W=[dst[ri].b])
            Bb = [mk(K, es, "Bb%d" % i, [128, 32, 16]) for i in range(2)]
            q1 = mk(K, es, "q1", [128, 32, 16])
            q2 = mk(K, es, "q2", [128, 32, 16])
            frb = sm["fr"].t[:, :].unsqueeze(2).to_broadcast([128, 32, 16])
            fib = sm["fi"].t[:, :].unsqueeze(2).to_broadcast([128, 32, 16])
            Br, Bi = Bt.t[:, 0, :, :], Bt.t[:, 1, :, :]
            O.tt(DVE, q1.t[:], Br, frb, ALU.mult, R=[Bt.b, Bf("fr")], W=[q1.b])
            O.tt(DVE, q2.t[:], Bi, fib, ALU.mult, R=[Bt.b, Bf("fi")], W=[q2.b])
            O.tt(DVE, Bb[0].t[:], q1.t[:], q2.t[:], ALU.subtract, R=[q1.b, q2.b], W=[Bb[0].b])
            O.tt(DVE, q1.t[:], Bi, frb, ALU.mult, R=[Bt.b, Bf("fr")], W=[q1.b])
            O.tt(DVE, q2.t[:], Br, fib, ALU.mult, R=[Bt.b, Bf("fi")], W=[q2.b])
            O.tt(DVE, Bb[1].t[:], q1.t[:], q2.t[:], ALU.add, R=[q1.b, q2.b], W=[Bb[1].b])
            T1 = mk(K, es, "T1", [128, 32, 128])
            T2 = mk(K, es, "T2", [128, 32, 128])
            T3 = mk(K, es, "T3", [128, 32, 128])
            T4 = mk(K, es, "T4", [128, 32, 128])
            T3b = mk(K, es, "T3b", [128, 32, 128], BF16)
            T4b = mk(K, es, "T4b", [128, 32, 128], BF16)

            def v4(tl):
                return tl.t[:, :, :].rearrange("p g (j c) -> p g j c", j=8)

            def cmul_outer(Pw, Xr, Xi, Xb, outRe, outIm, neg_im):
                pr = Pw[0].t[:, :, :].unsqueeze(3).to_broadcast([128, 32, 8, 16])
                pi_ = Pw[1].t[:, :, :].unsqueeze(3).to_broadcast([128, 32, 8, 16])
                xr = Xr.unsqueeze(2).to_broadcast([128, 32, 8, 16])
                xi = Xi.unsqueeze(2).to_broadcast([128, 32, 8, 16])
                O.tt(DVE, v4(T1), pr, xr, ALU.mult, R=[Pw[0].b] + Xb, W=[T1.b])
                O.tt(DVE, v4(T2), pi_, xi, ALU.mult, R=[Pw[1].b] + Xb, W=[T2.b])
                O.tt(POOL, outRe, T1.t[:], T2.t[:], ALU.subtract, R=[T1.b, T2.b], W=[outRe_b[0]])
                O.tt(DVE, v4(T1), pr, xi, ALU.mult, R=[Pw[0].b] + Xb, W=[T1.b])
                O.tt(DVE, v4(T2), pi_, xr, ALU.mult, R=[Pw[1].b] + Xb, W=[T2.b])
                if neg_im:
                    O.stt(outIm, T1.t[:], -1.0, T2.t[:], ALU.mult, ALU.subtract, R=[T1.b, T2.b], W=[outIm_b[0]])
                else:
                    O.tt(POOL, outIm, T1.t[:], T2.t[:], ALU.add, R=[T1.b, T2.b], W=[outIm_b[0]])

            outRe_b, outIm_b = [T3.b], [T4.b]
            cmul_outer(PC, Ct.t[:, 0, :, :], Ct.t[:, 1, :, :], [Ct.b], T3.t[:], T4.t[:], True)
            for d in range(2):
                hs = slice(d * 64, (d + 1) * 64)
                for ri, src in enumerate((T3, T4)):
                    O.memset(POOL, Wcm[d][ri].t[:], 0.0, W=[Wcm[d][ri].b])
                    O.act(Wcm[d][ri].t[hs], src.t[hs], AF.Copy, R=[src.b], W=[Wcm[d][ri].b])
            for ri in range(2):
                for d in range(2):
                    O.memset(POOL, Wbm[ri][d].t[:], 0.0, W=[Wbm[ri][d].b])
            outRe_b, outIm_b = [T3.b], [T4.b]
            cmul_outer(PB, Bb[0].t[:], Bb[1].t[:], [Bb[0].b, Bb[1].b], T3.t[:], T4.t[:], False)
            cnt = 0
            for ri, src in enumerate((T3, T4)):
                for g0 in range(0, 32, 4):
                    pb = K.ps[cnt % 2]
                    cnt += 1
                    for gg in range(4):
                        O.tr(pb.t[:, gg * 128:(gg + 1) * 128], src.t[:, g0 + gg, :], ident.t[:], R=[src.b, ident.b], W=[pb.b])
                    pv = pb.t[:, :].rearrange("p (g x) -> p g x", g=4)
                    for d in range(2):
                        O.act(Wbm[ri][d].t[:, g0:g0 + 4, d * 64:(d + 1) * 64], pv[:, :, d * 64:(d + 1) * 64], AF.Copy,
                              R=[pb.b], W=[Wbm[ri][d].b])
            outRe_b, outIm_b = [T3b.b], [T4b.b]
            cmul_outer(PNB, Bb[0].t[:], Bb[1].t[:], [Bb[0].b, Bb[1].b], T3b.t[:], T4b.t[:], False)
            cnt = 0
            for d in range(2):
                for g0 in range(0, 32, 4):
                    pb = K.ps[2 + cnt % 2]
                    cnt += 1
                    for gg in range(4):
                        g = g0 + gg
                        O.mm(pb.t[:, gg * 128:(gg + 1) * 128], T3b.t[:, g, :], Wcm[d][0].t[:, g, :], True, False,
                             R=[T3b.b, Wcm[d][0].b], W=[pb.b])
                        O.mm(pb.t[:, gg * 128:(gg + 1) * 128], T4b.t[:, g, :], Wcm[d][1].t[:, g, :], False, True,
                             R=[T4b.b, Wcm[d][1].b], W=[pb.b])
                    O.tt(DVE, M[d].t[:, g0:g0 + 4, :], pb.t[:, :].rearrange("p (g x) -> p g x", g=4),
                         CM.t[:, d:d + 1, :].to_broadcast([128, 4, 128]), ALU.mult, R=[pb.b, CM.b], W=[M[d].b])
            end_phase(K)
        with ExitStack() as es:
            angt = mk(K, es, "angt", [128, NCH])
            kft = mk(K, es, "kft", [128, NCH]); rrt = mk(K, es, "rrt", [128, NCH]); mmt = mk(K, es, "mmt", [128, NCH])
            kit = Tl(es.enter_context(nc.sbuf_tensor("kit", [128, NCH], mybir.dt.int32)), "kit")
            sTt = [mk(K, es, "sTt%d" % i, [128, NCH]) for i in range(2)]
            cTt = [mk(K, es, "cTt%d" % i, [128, NCH]) for i in range(2)]
            for g in range(32):
                s_, c_ = sTt[g % 2], cTt[g % 2]
                O.ts(DVE, angt.t[:], nvec.t[:], ph8.t[:, g:g + 1], ALU.mult, R=[nvec.b, ph8.b], W=[angt.b])
                sincos(K, O, angt, s_, c_, (kft, rrt, mmt), kit, lambda tl: tl.t[:])
                O.dma(K.TBL.ap()[g, 0], s_.t[:], R=[s_.b], W=[K.bTBL])
                O.dma(K.TBL.ap()[g, 1], c_.t[:], R=[c_.b], W=[K.bTBL])
            end_phase(K)
        with ExitStack() as es:
            nb_b = NB if "b" in K.s5_parts else 0
            Ulr = [mk(K, es, "Ulr%d" % m, [128, 8, 128]) for m in range(2)]
            Ul = [mk(K, es, "Ul%d" % m, [128, 8, 128]) for m in range(4)]
            Uc = mk(K, es, "Uc", [128, 8, 128])
            O.memset(POOL, Uc.t[:], 0.0, W=[Uc.b])
            Yt = [mk(K, es, "Yt%d" % m, [128, 8, 128]) for m in range(4)]
            gout = Ulr
            perm_in = lambda tl: tl.t[:, :, :].rearrange("p i (g c) -> p g i c", g=8)
            v_gic = lambda tl: tl.t[:, :, :].rearrange("p g (i c) -> p g i c", i=8)
            Dbc = mk(K, es, "Dbc", [128, 512])
            Ugf = [mk(K, es, "Ugf%d" % i, [128, NCH], BF16) for i in range(2)]
            Ugr = [mk(K, es, "Ugr%d" % i, [128, NCH], BF16) for i in range(2)]
            Ssbs = [mk(K, es, "Ssb%d" % i, [128, 2, NCH]) for i in range(2)]
            Zs = [mk(K, es, "Z%d" % i, [128, 2, NCH]) for i in range(2)]
            Qs = [mk(K, es, "Q%d" % i, [128, 2, NCH]) for i in range(2)]
            Hs = [mk(K, es, "H%d" % i, [128, 2, NCH], BF16) for i in range(2)]
            a1 = mk(K, es, "a1", [128, NCH]); a2 = mk(K, es, "a2", [128, NCH])
            a3 = mk(K, es, "a3", [128, NCH]); a4 = mk(K, es, "a4", [128, NCH])
            cTs = [mk(K, es, "cTb%d" % i, [128, NCH]) for i in range(2)]
            sTs = [mk(K, es, "sTb%d" % i, [128, NCH]) for i in range(2)]
            Yr = mk(K, es, "Yr", [128, 512])
            Yg = mk(K, es, "Yg", [128, 512])
            g1 = mk(K, es, "g1", [128, 8, 128]); g2 = mk(K, es, "g2", [128, 8, 128])
            O.dma(Dbc.t[:], K.s5D.ap().partition_broadcast(128), W=[Dbc.b])
            pA, pB_, SL0, SL1, SCb, PY0, PY1, TO = K.ps
            SL = [SL0, SL1]
            PY = [PY0, PY1]
            it = 0
            for b in range(nb_b):
                for qtr in range(4):
                    if K.s5_limit is not None and (b, qtr) != (0, 0):
                        continue
                    cs = slice(qtr * 128, (qtr + 1) * 128)
                    for m in range(5):
                        raw = Ulr[m % 2]
                        if m < 4:
                            O.dma(raw.t[:], K.U.ap()[b, LC + 1024 * m:LC + 1024 * (m + 1), cs].rearrange("(n i) c -> n i c", i=8),
                                  R=[K.bU], W=[raw.b])
                            O.act(v_gic(Ul[m]), perm_in(raw), AF.Copy, R=[raw.b], W=[Ul[m].b])
                        else:
                            O.dma(raw.t[0:32], K.U.ap()[b, 0:LC, cs].rearrange("(n i) c -> n i c", i=8), R=[K.bU], W=[raw.b])
                            O.act(Uc.t[0:32, :, :].rearrange("p g (i c) -> p g i c", i=8),
                                  raw.t[0:32, :, :].rearrange("p i (g c) -> p g i c", g=8), AF.Copy, R=[raw.b], W=[Uc.b])
                    for gl in range(8):
                        g = qtr * 8 + gl
                        uf, ur = Ugf[it % 2], Ugr[it % 2]
                        Ssb, Z, Q, H, cT, sT = Ssbs[it % 2], Zs[it % 2], Qs[it % 2], Hs[it % 2], cTs[it % 2], sTs[it % 2]
                        it += 1
                        O.tr(pB_.t[:, 0:128], Uc.t[:, gl, :], ident.t[:], R=[Uc.b, ident.b], W=[pB_.b])
                        for m in range(4):
                            O.tr(pA.t[:, m * 128:(m + 1) * 128], Ul[m].t[:, gl, :], ident.t[:], R=[Ul[m].b, ident.b], W=[pA.b])
                        O.act(uf.t[:, 0:NCC], pB_.t[:, 0:NCC], AF.Copy, R=[pB_.b], W=[uf.b])
                        O.act(uf.t[:, NCC:NCH], pA.t[:], AF.Copy, R=[pA.b], W=[uf.b])
                        O.copy(DVE, ur.t[:, 0:NCC][:, ::-1], uf.t[:, 0:NCC], R=[uf.b], W=[ur.b])
                        O.copy(DVE, ur.t[:, NCC:NCH][:, ::-1], uf.t[:, NCC:NCH], R=[uf.b], W=[ur.b])
                        ud = [uf, ur]
                        for ri in range(2):
                            for d in range(2):
                                Wb = Wbm[ri][d]
                                O.mm(SL[ri].t[:], Wb.t[:, g, :], ud[d].t[:, NCC:NCH], d == 0, d == 1,
                                     R=[Wb.b, ud[d].b], W=[SL[ri].b])
                            for d in range(2):
                                Wb = Wbm[ri][d]
                                O.mm(SCb.t[:, ri * NCC:(ri + 1) * NCC], Wb.t[:, g, :], ud[d].t[:, 0:NCC], d == 0, d == 1,
                                     R=[Wb.b, ud[d].b], W=[SCb.b])
                        for ri in range(2):
                            O.act(Ssb.t[:, ri, NCC:NCH], SL[ri].t[:], AF.Copy, R=[SL[ri].b], W=[Ssb.b])
                            O.act(Ssb.t[:, ri, 0:NCC], SCb.t[:, ri * NCC:(ri + 1) * NCC], AF.Copy, R=[SCb.b], W=[Ssb.b])
                        O.dma(sT.t[:], K.TBL.ap()[g, 0], R=[K.bTBL], W=[sT.b])
                        O.dma(cT.t[:], K.TBL.ap()[g, 1], R=[K.bTBL], W=[cT.b])
                        Sre, Sim = Ssb.t[:, 0, :], Ssb.t[:, 1, :]
                        O.tt(DVE, a1.t[:], Sre, cT.t[:], ALU.mult, R=[Ssb.b, cT.b], W=[a1.b])
                        O.tt(POOL, a2.t[:], Sim, sT.t[:], ALU.mult, R=[Ssb.b, sT.b], W=[a2.b])
                        O.tt(DVE, Z.t[:, 0, :], a1.t[:], a2.t[:], ALU.add, R=[a1.b, a2.b], W=[Z.b])
                        O.tt(POOL, a3.t[:], Sim, cT.t[:], ALU.mult, R=[Ssb.b, cT.b], W=[a3.b])
                        O.tt(DVE, a4.t[:], Sre, sT.t[:], ALU.mult, R=[Ssb.b, sT.b], W=[a4.b])
                        O.tt(DVE, Z.t[:, 1, :], a3.t[:], a4.t[:], ALU.subtract, R=[a3.b, a4.b, Z.b], W=[Z.b])
                        rb = r8.t[:, g:g + 1].to_broadcast([128, NCH])
                        for ri in range(2):
                            zz, qq = Z.t[:, ri, :], Q.t[:, ri, :]
                            K.P.op(DVE, (lambda zz, qq, rb: (lambda e: e.tensor_tensor_scan(out=qq, data0=rb, data1=zz, initial=0.0,
                                                                                         op0=ALU.mult, op1=ALU.add)))(zz, qq, rb),
                                   [Z.b, r8.b], [Q.b])
                        Qre, Qim = Q.t[:, 0, :], Q.t[:, 1, :]
                        O.tt(DVE, a1.t[:], Qre, cT.t[:], ALU.mult, R=[Q.b, cT.b], W=[a1.b])
                        O.tt(POOL, a2.t[:], Qim, sT.t[:], ALU.mult, R=[Q.b, sT.b], W=[a2.b])
                        O.tt(DVE, H.t[:, 0, :], a1.t[:], a2.t[:], ALU.subtract, R=[a1.b, a2.b], W=[H.b])
                        O.tt(POOL, a3.t[:], Qim, cT.t[:], ALU.mult, R=[Q.b, cT.b], W=[a3.b])
                        O.tt(DVE, a4.t[:], Qre, sT.t[:], ALU.mult, R=[Q.b, sT.b], W=[a4.b])
                        O.tt(DVE, H.t[:, 1, :], a3.t[:], a4.t[:], ALU.add, R=[a3.b, a4.b, H.b], W=[H.b])
                        for d in range(2):
                            O.mm(PY[d].t[:], M[d].t[:, g, :], ud[d].t[:, NCC:NCH], True, False, R=[M[d].b, ud[d].b], W=[PY[d].b])
                            O.mm(PY[d].t[:], Wcm[d][0].t[:, g, :], H.t[:, 0, NCC - 1:NCH - 1], False, False,
                                 R=[Wcm[d][0].b, H.b], W=[PY[d].b])
                            O.mm(PY[d].t[:], Wcm[d][1].t[:, g, :], H.t[:, 1, NCC - 1:NCH - 1], False, True,
                                 R=[Wcm[d][1].b, H.b], W=[PY[d].b])
                        O.copy(DVE, Yr.t[:, ::-1], PY[1].t[:], R=[PY[1].b], W=[Yr.b])
                        O.tt(DVE, Yg.t[:], PY[0].t[:], Yr.t[:], ALU.add, R=[PY[0].b, Yr.b], W=[Yg.b])
                        for m in range(4):
                            O.tr(TO.t[:, m * 128:(m + 1) * 128], Yg.t[:, m * 128:(m + 1) * 128], ident.t[:], R=[Yg.b, ident.b], W=[TO.b])
                        for m in range(4):
                            O.act(Yt[m].t[:, gl, :], TO.t[:, m * 128:(m + 1) * 128], AF.Copy, R=[TO.b], W=[Yt[m].b])
                    dv = Dbc.t[:, cs].rearrange("p (g c) -> p g c", g=8).unsqueeze(2).to_broadcast([128, 8, 8, 16])
                    for m in range(4):
                        y = Yt[m]
                        go = gout[m % 2]
                        O.tt(DVE, v_gic(g1), v_gic(Ul[m]), dv, ALU.mult, R=[Ul[m].b, Dbc.b], W=[g1.b])
                        O.tt(POOL, y.t[:], y.t[:], g1.t[:], ALU.add, R=[y.b, g1.b], W=[y.b])
                        O.tt(POOL, g1.t[:], y.t[:], y.t[:], ALU.mult, R=[y.b], W=[g1.b])
                        O.ts(DVE, g1.t[:], g1.t[:], 0.044715, ALU.mult, 1.0, ALU.add, R=[g1.b], W=[g1.b])
                        O.tt(POOL, g2.t[:], g1.t[:], y.t[:], ALU.mult, R=[g1.b, y.b], W=[g2.b])
                        O.act(g2.t[:], g2.t[:], AF.Sigmoid, R=[g2.b], W=[g2.b], scale=1.5957691216057308)
                        O.tt(DVE, perm_in(go), v_gic(g2), v_gic(y), ALU.mult, R=[g2.b, y.b], W=[go.b])
                        O.dma(K.YG.ap()[b, 1024 * m:1024 * (m + 1), cs].rearrange("(n i) c -> n i c", i=8), go.t[:], R=[go.b], W=[K.bYG])
            end_phase(K)
    with ExitStack() as es:
        if "c" not in K.s5_parts:
            return
        wg = mk(K, es, "wg", [128, 4, 512], BF16)
        bg = mk(K, es, "bg", [128, 4])
        yt = [mk(K, es, "ytk%d" % i, [128, 512]) for i in range(2)]
        yTf = mk(K, es, "yTf", [128, 4, 512])
        yTb = mk(K, es, "yTb", [128, 4, 512], BF16)
        sg = [mk(K, es, "sg%d" % i, [128, 512]) for i in range(2)]
        SXs = [mk(K, es, "SXs%d" % i, [128, 4, 512], BF16) for i in range(2)]
        ident = K.ident
        O.dma(wg.t[:], K.w_glu.ap().rearrange("(k p) c -> p k c", p=128), W=[wg.b], eng=POOL)
        O.dma(bg.t[:], K.b_glu4.ap(), W=[bg.b])
        n = 0
        for b in range(NB):
            for blk in range(L // 512):
                tok0 = blk * 512
                for t in range(4):
                    y_ = yt[t % 2]
                    O.dma(y_.t[:], K.YG.ap()[b, tok0 + t * 128:tok0 + (t + 1) * 128, :], R=[K.bYG], W=[y_.b])
                    for k in range(4):
                        O.tr(K.ps[k].t[:, t * 128:(t + 1) * 128], y_.t[:, k * 128:(k + 1) * 128], ident.t[:], R=[y_.b, ident.b], W=[K.ps[k].b])
                for k in range(4):
                    O.copy(DVE, yTf.t[:, k, :], K.ps[k].t[:], R=[K.ps[k].b], W=[yTf.b])
                    O.act(yTb.t[:, k, :], yTf.t[:, k, :], AF.Copy, R=[yTf.b], W=[yTb.b])
                sx = SXs[n % 2]
                n += 1
                for co in range(4):
                    pz = K.ps[4 + co % 2]
                    for k in range(4):
                        O.mm(pz.t[:], wg.t[:, k, co * 128:(co + 1) * 128], yTb.t[:, k, :], k == 0, k == 3, R=[wg.b, yTb.b], W=[pz.b])
                    s_ = sg[co % 2]
                    O.act(s_.t[:], pz.t[:], AF.Sigmoid, R=[pz.b, bg.b], W=[s_.b], bias=bg.t[:, co:co + 1])
                    O.tt(DVE, sx.t[:, co, :], yTf.t[:, co, :], s_.t[:], ALU.mult, R=[yTf.b, s_.b], W=[sx.b])
                O.dma(K.SXT.ap()[b].rearrange("(k p) t -> p k t", p=128)[:, :, tok0:tok0 + 512], sx.t[:], R=[sx.b], W=[K.bSXT])
        end_phase(K)


def phase_merge(K):
    nc, O = K.nc, K.O
    with ExitStack() as es:
        wpa = mk(K, es, "wpa", [128, 8, D], BF16)
        wps = mk(K, es, "wps", [128, 4, D], BF16)
        wout = mk(K, es, "wout", [128, 8, D], BF16)
        GT = mk(K, es, "GT2", [128, D])
        OTb = [mk(K, es, "OTb%d" % i, [128, 8, 512], BF16) for i in range(2)]
        SXb = [mk(K, es, "SXb%d" % i, [128, 4, 512], BF16) for i in range(2)]
        GAb = [mk(K, es, "GAb%d" % i, [128, 8, 512]) for i in range(2)]
        GSb = [mk(K, es, "GSb%d" % i, [128, 8, 512]) for i in range(2)]
        mT = mk(K, es, "mT", [128, 8, 512], BF16)
        ta = [mk(K, es, "ta%d" % i, [128, 512]) for i in range(2)]
        tb = [mk(K, es, "tb%d" % i, [128, 512]) for i in range(2)]
        xr = [mk(K, es, "xrm%d" % i, [128, D]) for i in range(2)]
        tmp = mk(K, es, "tmpm", [128, D])
        O.dma(wpa.t[:], K.w_pa.ap().rearrange("(k p) c -> p k c", p=128), W=[wpa.b], eng=POOL)
        O.dma(wps.t[:], K.w_ps.ap().rearrange("(k p) c -> p k c", p=128), W=[wps.b], eng=POOL)
        O.dma(wout.t[:], K.w_out.ap().rearrange("(k p) c -> p k c", p=128), W=[wout.b], eng=POOL)
        bi = 0
        for b in range(NB):
            O.dma(GT.t[:], K.MOD.ap()[b:b + 1, 5 * D:6 * D].partition_broadcast(128), R=[K.bMOD], W=[GT.b])
            for blk in range(L // 512):
                tok0 = blk * 512
                ot, sx, ga, gs = OTb[bi % 2], SXb[bi % 2], GAb[bi % 2], GSb[bi % 2]
                bi += 1
                O.dma(ot.t[:], K.OT.ap()[b].rearrange("(k p) t -> p k t", p=128)[:, :, tok0:tok0 + 512], R=[K.bOT], W=[ot.b])
                O.dma(sx.t[:], K.SXT.ap()[b].rearrange("(k p) t -> p k t", p=128)[:, :, tok0:tok0 + 512], R=[K.bSXT], W=[sx.b])
                O.dma(ga.t[:], K.GA.ap()[b].rearrange("(k p) t -> p k t", p=128)[:, :, tok0:tok0 + 512], R=[K.bG], W=[ga.b])
                O.dma(gs.t[:], K.GS.ap()[b].rearrange("(k p) t -> p k t", p=128)[:, :, tok0:tok0 + 512], R=[K.bG], W=[gs.b])
                for m in range(8):
                    ppa, pps = K.ps[(m % 2) * 2], K.ps[(m % 2) * 2 + 1]
                    for k in range(8):
                        O.mm(ppa.t[:], wpa.t[:, k, m * 128:(m + 1) * 128], ot.t[:, k, :], k == 0, k == 7, R=[wpa.b, ot.b], W=[ppa.b])
                    for k in range(4):
                        O.mm(pps.t[:], wps.t[:, k, m * 128:(m + 1) * 128], sx.t[:, k, :], k == 0, k == 3, R=[wps.b, sx.b], W=[pps.b])
                    a_, b_ = ta[m % 2], tb[m % 2]
                    O.tt(DVE, a_.t[:], ppa.t[:], ga.t[:, m, :], ALU.mult, R=[ppa.b, ga.b], W=[a_.b])
                    O.tt(DVE, b_.t[:], pps.t[:], gs.t[:, m, :], ALU.mult, R=[pps.b, gs.b], W=[b_.b])
                    O.tt(POOL, mT.t[:, m, :], a_.t[:], b_.t[:], ALU.add, R=[a_.b, b_.b], W=[mT.b])
                for t in range(4):
                    xo = xr[t % 2]
                    r0 = tok0 + t * 128
                    O.dma(xo.t[:], K.X1.ap()[b, r0:r0 + 128, :], R=[K.bX1], W=[xo.b])
                    for h in range(2):
                        px = K.ps[4 + h]
                        for k in range(8):
                            O.mm(px.t[:], mT.t[:, k, t * 128:(t + 1) * 128], wout.t[:, k, h * 512:(h + 1) * 512], k == 0, k == 7,
                                 R=[mT.b, wout.b], W=[px.b])
                        O.tt(DVE, tmp.t[:, h * 512:(h + 1) * 512], px.t[:], GT.t[:, h * 512:(h + 1) * 512], ALU.mult,
                             R=[px.b, GT.b], W=[tmp.b])
                    O.tt(POOL, xo.t[:], xo.t[:], tmp.t[:], ALU.add, R=[xo.b, tmp.b], W=[xo.b])
                    O.dma(K.X2.ap()[b, r0:r0 + 128, :], xo.t[:], R=[xo.b], W=[K.bX2])
        end_phase(K)


def build(debug=False, phases=("mod", "ffn1", "inproj", "attn", "s5", "merge", "ffn2"), ext=()):
    nc = bass.Bass("TRN2", target_bir_lowering=False)
    K = Ctx()
    K.nc = nc
    K.debug = debug
    K.s5_parts = "abc"
    K.dbg_names = debug if isinstance(debug, (list, tuple, set)) else ()
    K.s5_limit = None
    for p_ in phases:
        if p_.startswith("s5:"):
            K.s5_parts = p_[3:].replace("1", "")
            K.s5_limit = 1 if "1" in p_ else None
    if any(p_.startswith("s5:") for p_ in phases):
        phases = tuple(phases) + ("s5",)

    def din(name, shape, dtype=F32):
        return nc.dram_tensor(name, list(shape), dtype, kind="ExternalInput")

    def scr(name, shape, dtype=F32):
        if name in ext:
            return nc.dram_tensor(name, list(shape), dtype, kind="ExternalInput")
        return nc.dram_tensor(name, list(shape), dtype, kind="ExternalOutput" if (debug and name in K.dbg_names) else "Internal")

    K.x = din("x", [NB, L, D])
    K.ctx = din("ctx", [NB, LC, D])
    K.cT = din("cT", [128, 8, 3])
    K.w_mod = din("w_mod", [D, 9 * D])
    K.b_mod = din("b_mod", [1, 9 * D])
    K.norm_ffn1 = din("norm_ffn1", [1, D])
    K.w13_ffn1 = din("w13_ffn1", [D, 2 * DFF])
    K.w2_ffn1 = din("w2_ffn1", [DFF, D])
    K.norm_mix = din("norm_mix", [1, D])
    K.w_in = din("w_in", [D, 5632])
    K.q_norm = din("q_norm", [1, 64])
    K.k_norm = din("k_norm", [1, 64])
    K.lamv = din("lamv", [1, 256])
    K.subln = din("subln", [128, 1])
    K.norm_ffn2 = din("norm_ffn2", [1, D])
    K.w13_ffn2 = din("w13_ffn2", [D, 2 * DFF])
    K.w2_ffn2 = din("w2_ffn2", [DFF, D])
    K.w_pa = din("w_pa", [D, D])
    K.w_ps = din("w_ps", [512, D])
    K.w_out = din("w_out", [D, D])
    K.w_glu = din("w_glu", [512, 512])
    K.identd = din("identd", [128, 128])
    K.s5p = din("s5p", [128, 3, 32])
    K.s5B = din("s5B", [128, 2, 32, 16])
    K.s5C = din("s5C", [128, 2, 32, 16])
    K.s5D = din("s5D", [1, 512])
    K.CMd = din("CMd", [128, 2, 128])
    K.nvecd = din("nvecd", [128, LK // 8])
    K.b_glu4 = din("b_glu4", [128, 4])
    K.ropec = din("ropec", [L, 32])
    K.ropes = din("ropes", [L, 32])
    K.out = nc.dram_tensor("out", [NB, L, D], F32, kind="ExternalOutput")

    K.MOD = scr("MOD", [3, 9 * D])
    K.X1 = scr("X1", [NB, L, D])
    K.C1 = scr("C1", [NB, LC, D])
    K.QT = scr("QT", [NB, D, L], BF16)
    K.KT = scr("KT", [NB, D, LK], BF16)
    K.V = scr("V", [NB, LK, D], BF16)
    K.U = scr("U", [NB, LK, 512])
    K.GA = scr("GA", [NB, D, L])
    K.GS = scr("GS", [NB, D, L])
    K.OT = scr("OT", [NB, D, L], BF16)
    K.SXT = scr("SXT", [NB, 512, L], BF16)
    K.X2 = scr("X2", [NB, L, D])
    K.YG = scr("YG", [NB, L, 512])
    K.TBL = scr("TBL", [32, 2, 128, LK // 8])
    K.bTBL = Buf()
    K.bYG = Buf()
    K.bMOD, K.bX1, K.bC1, K.bOUT, K.bIN = Buf(), Buf(), Buf(), Buf(), Buf()
    K.bQT, K.bKT, K.bV, K.bU, K.bG, K.bOT, K.bSXT, K.bX2 = [Buf() for _ in range(8)]

    with ExitStack() as es0:
        K.sems = {e: es0.enter_context(nc.semaphore("s_" + e)) for e in ENGS}
        K.dsems = [es0.enter_context(nc.semaphore("d%d" % i)) for i in range(NDMA + NDMA_P)]
        K.P = Prog(nc)
        K.O = Ops(K.P)
        K.ps = [Tl(es0.enter_context(nc.psum_tensor("ps%d" % i, [128, 512], F32)), "ps%d" % i) for i in range(8)]
        for p_ in K.ps:
            p_.b.excl = True
        K.ident = mk(K, es0, "ident", [128, 128])
        K.O.dma(K.ident.t[:], K.identd.ap(), W=[K.ident.b])
        if "mod" in phases:
            phase_mod(K)
        if "ffn1" in phases:
            phase_ffn(K, "a", K.w13_ffn1, K.w2_ffn1, K.norm_ffn1, 0,
                      ffn_blocks(K.x, K.X1, K.ctx, K.C1), K.bIN, K.bX1)
        if "inproj" in phases:
            phase_inproj(K)
        if "attn" in phases:
            phase_attn(K)
        if "s5" in phases:
            phase_s5(K)
        if "merge" in phases:
            phase_merge(K)
        if "ffn2" in phases:
            phase_ffn(K, "b", K.w13_ffn2, K.w2_ffn2, K.norm_ffn2, 6, ffn_blocks(K.X2, K.out), K.bX2, K.bOUT)
        end_phase(K)
    return nc, K


def host_inputs(inputs):
    f = lambda a: np.ascontiguousarray(np.asarray(a, dtype=np.float32))
    sq = lambda k: f(inputs[k])[0]
    common = {
        "w_mod": sq("w_mod"), "b_mod": sq("b_mod")[None, :],
        "norm_ffn1": sq("norm_ffn1")[None, :], "w13_ffn1": sq("w13_ffn1"), "w2_ffn1": sq("w2_ffn1"),
        "norm_mix": sq("norm_mix")[None, :], "w_in": sq("w_in"),
        "q_norm": sq("q_norm")[None, :], "k_norm": sq("k_norm")[None, :],
        "lamv": np.concatenate([sq("lam_q1"), sq("lam_k1"), sq("lam_q2"), sq("lam_k2")])[None, :],
        "subln": sq("subln")[:, None],
        "norm_ffn2": sq("norm_ffn2")[None, :], "w13_ffn2": sq("w13_ffn2"), "w2_ffn2": sq("w2_ffn2"),
        "w_pa": sq("w_pa"), "w_ps": sq("w_ps"), "w_out": sq("w_out"), "w_glu": sq("w_glu"),
        "b_glu4": np.ascontiguousarray(sq("b_glu").reshape(4, 128).T),
        "identd": np.eye(128, dtype=np.float32),
    }
    lre, lim, ldt = sq("s5_lam_re"), sq("s5_lam_im"), sq("s5_log_dt")
    s5p = np.zeros((128, 3, 32), np.float32)
    s5p[:, 0, :] = lre.transpose(0, 2, 1).reshape(128, 32)
    s5p[:, 1, :] = lim.transpose(0, 2, 1).reshape(128, 32)
    s5p[:, 2, :] = np.repeat(ldt[:, None, :], 64, axis=1).reshape(128, 32)
    common["s5p"] = s5p
    bre, bim = sq("s5_b_re"), sq("s5_b_im")
    common["s5B"] = np.ascontiguousarray(np.stack([bre.transpose(0, 2, 1, 3).reshape(128, 32, 16),
                                                   bim.transpose(0, 2, 1, 3).reshape(128, 32, 16)], axis=1))
    cre, cim = sq("s5_c_re"), sq("s5_c_im")
    common["s5C"] = np.ascontiguousarray(np.stack([cre.transpose(0, 3, 1, 2).reshape(128, 32, 16),
                                                   cim.transpose(0, 3, 1, 2).reshape(128, 32, 16)], axis=1))
    common["s5D"] = sq("s5_d")[None, :]
    jj = np.repeat(np.arange(8), 16)
    cm = np.zeros((128, 2, 128), np.float32)
    cm[:, 0, :] = (jj[None, :] >= jj[:, None])
    cm[:, 1, :] = (jj[:, None] >= jj[None, :])
    common["CMd"] = cm
    common["nvecd"] = np.ascontiguousarray(np.broadcast_to(np.arange(LK // 8, dtype=np.float32), (128, LK // 8)))
    n_freq = 16
    inv = (10000.0 ** (-np.arange(n_freq, dtype=np.float32) / n_freq)).astype(np.float32)
    row = np.repeat(np.arange(64, dtype=np.float32), 64)
    col = np.tile(np.arange(64, dtype=np.float32), 64)
    ang = np.concatenate([row[:, None] * inv, col[:, None] * inv], axis=-1).astype(np.float32)
    common["ropec"] = np.cos(ang).astype(np.float32)
    common["ropes"] = np.sin(ang).astype(np.float32)
    x = f(inputs["x"])
    c = f(inputs["c"])
    ctx = f(inputs["ctx"])
    cc = f(inputs["c_ctx"])
    maps = []
    for i in range(8):
        b0 = NB * i
        cols = np.stack([c[b0], c[b0 + 1], cc], axis=-1)
        cT = np.ascontiguousarray(cols.reshape(8, 128, 3).transpose(1, 0, 2))
        m = dict(common)
        m["x"] = x[b0:b0 + NB]
        m["ctx"] = ctx[b0:b0 + NB]
        m["cT"] = cT
        maps.append(m)
    return maps


def kernel(**inputs):
    nc, K = build()
    maps = host_inputs(inputs)
    res = run_bass_kernel_spmd(nc, maps, core_ids=list(range(8)))
    out = np.concatenate([np.asarray(r["out"]) for r in res.results], axis=0)
    return out.astype(np.float32)
```

```python
import math
import numpy as np
import concourse.bass as bass
import concourse.mybir as mybir
from concourse.bass_utils import run_bass_kernel_spmd
from contextlib import ExitStack

F32 = mybir.dt.float32
BF16 = mybir.dt.bfloat16
AF = mybir.ActivationFunctionType
ALU = mybir.AluOpType
AX = mybir.AxisListType

PE, ACT, DVE, POOL, SP = "pe", "act", "dve", "pool", "sp"
ENGS = (PE, ACT, DVE, POOL, SP)
NDMA = 12
NDMA_P = 4

D = 1024
L = 4096
LC = 256
LK = L + LC
DFF = 2816
NB = 2
EPS = 1e-6
LAM_INIT = 0.2


class Buf:
    __slots__ = ("name", "w", "r", "excl")

    def __init__(self, name=""):
        self.name = name
        self.w = None
        self.r = {}
        self.excl = False


class Op:
    __slots__ = ("eng", "idx", "fn", "deps", "target", "semval", "dma", "dsem", "dval")

    def __init__(self, eng, idx, fn, dma):
        self.eng, self.idx, self.fn, self.dma = eng, idx, fn, dma
        self.deps = []
        self.target = False
        self.semval = None
        self.dsem = None
        self.dval = None


class Prog:
    def __init__(self, nc):
        self.nc = nc
        self.q = {e: [] for e in ENGS}
        self.ndma = 0
        self.ndma_p = 0
        self.dma_last = [None] * (NDMA + NDMA_P)
        self.done = {e: 0 for e in ENGS}
        self.cnt = {e: 0 for e in ENGS}
        self.known = {e: {} for e in ENGS}

    def op(self, eng, fn, reads=(), writes=(), dma=False):
        o = Op(eng, len(self.q[eng]), fn, dma)
        deps = {}

        def add(p):
            if p is None:
                return
            if (not p.dma) and p.idx < self.done[p.eng]:
                return
            if (not p.dma) and p.eng == eng and not dma and eng == PE:
                return
            deps[id(p)] = p

        for b in reads:
            if b.excl:
                for p in b.r.values():
                    if p.eng != eng:
                        add(p)
            p = b.w
            if p is not None:
                if (not p.dma) and p.idx < self.done[p.eng]:
                    pass
                elif (not p.dma) and p.eng == eng and not dma:
                    deps[id(p)] = p
                else:
                    add(p)
        for b in writes:
            add(b.w)
            for p in b.r.values():
                add(p)
        if dma:
            if eng == POOL:
                slot = NDMA + self.ndma_p % NDMA_P
                o.dval = 16 * (self.ndma_p // NDMA_P + 1)
                self.ndma_p += 1
            else:
                slot = self.ndma % NDMA
                o.dval = 16 * (self.ndma // NDMA + 1)
                self.ndma += 1
            prev = self.dma_last[slot]
            if prev is not None:
                deps[id(prev)] = prev
            self.dma_last[slot] = o
            o.dsem = slot
        o.deps = list(deps.values())
        for p in o.deps:
            p.target = True
        self.q[eng].append(o)
        for b in reads:
            key = ("d", id(o)) if dma else eng
            b.r[key] = o
        for b in writes:
            b.w = o
            b.r = {}
        return o

    def barrier(self):
        lasts = []
        for e in ENGS:
            for o in reversed(self.q[e][self.done[e]:]):
                if o.fn is not None and not o.dma:
                    lasts.append(o)
                    break
        dmas = [d for d in self.dma_last if d is not None]
        for e in ENGS:
            o = Op(e, len(self.q[e]), None, False)
            o.deps = [p for p in lasts if p.eng != e] + dmas
            for p in o.deps:
                p.target = True
            self.q[e].append(o)

    def emit(self, block, sems, dsems):
        self.barrier()
        for e in ENGS:
            for o in self.q[e][self.done[e]:]:
                if o.target and not o.dma:
                    self.cnt[e] += 1
                    o.semval = self.cnt[e]
        prog = self

        def run(e, engine):
            known = prog.known[e]
            ops = prog.q[e][prog.done[e]:]
            prog.done[e] = len(prog.q[e])
            for o in ops:
                need = {}
                for p in o.deps:
                    if p.dma:
                        k, v = ("d", p.dsem), p.dval
                    else:
                        k, v = p.eng, p.semval
                    if need.get(k, 0) < v:
                        need[k] = v
                for k, v in need.items():
                    if known.get(k, 0) >= v:
                        continue
                    known[k] = v
                    s = dsems[k[1]] if isinstance(k, tuple) else sems[k]
                    engine.wait_ge(s, v)
                if o.fn is None:
                    continue
                ins = o.fn(engine)
                if o.dma:
                    ins.then_inc(dsems[o.dsem], 16)
                elif o.target:
                    ins.then_inc(sems[e], 1)

        @block.tensor
        def _(eng):
            run(PE, eng)

        @block.scalar
        def _(eng):
            run(ACT, eng)

        @block.vector
        def _(eng):
            run(DVE, eng)

        @block.gpsimd
        def _(eng):
            run(POOL, eng)

        @block.sync
        def _(eng):
            run(SP, eng)


class Tl:
    __slots__ = ("t", "b")

    def __init__(self, t, name=""):
        self.t = t
        self.b = Buf(name)


class Ctx:
    pass


class Ops:
    def __init__(self, P):
        self.P = P

    def dma(self, out, in_, R=(), W=(), eng=SP):
        return self.P.op(eng, lambda e: e.dma_start(out=out, in_=in_), R, W, dma=True)

    def mm(self, out, lhsT, rhs, start, stop, R=(), W=()):
        return self.P.op(PE, lambda e: e.matmul(out, lhsT, rhs, start=start, stop=stop), R, W)

    def tr(self, out, in_, ident, R=(), W=()):
        return self.P.op(PE, lambda e: e.transpose(out=out, in_=in_, identity=ident), R, W)

    def act(self, out, in_, func, R=(), W=(), bias=None, scale=None, accum_out=None):
        kw = {}
        if bias is not None:
            kw["bias"] = bias
        if scale is not None:
            kw["scale"] = scale
        if accum_out is not None:
            kw["accum_out"] = accum_out
        return self.P.op(ACT, lambda e: e.activation(out=out, in_=in_, func=func, **kw), R, W)

    def tt(self, eng, out, in0, in1, op, R=(), W=()):
        return self.P.op(eng, lambda e: e.tensor_tensor(out=out, in0=in0, in1=in1, op=op), R, W)

    def ts(self, eng, out, in0, s1, op0, s2=None, op1=None, R=(), W=()):
        if op1 is None:
            return self.P.op(eng, lambda e: e.tensor_scalar(out=out, in0=in0, scalar1=s1, scalar2=None, op0=op0), R, W)
        return self.P.op(eng, lambda e: e.tensor_scalar(out=out, in0=in0, scalar1=s1, scalar2=s2, op0=op0, op1=op1), R, W)

    def stt(self, out, in0, scalar, in1, op0, op1, R=(), W=()):
        return self.P.op(DVE, lambda e: e.scalar_tensor_tensor(out=out, in0=in0, scalar=scalar, in1=in1, op0=op0, op1=op1), R, W)

    def red(self, out, in_, op, R=(), W=()):
        return self.P.op(DVE, lambda e: e.tensor_reduce(out=out, in_=in_, axis=AX.X, op=op), R, W)

    def recip(self, out, in_, R=(), W=()):
        return self.P.op(DVE, lambda e: e.reciprocal(out=out, in_=in_), R, W)

    def copy(self, eng, out, in_, R=(), W=()):
        return self.P.op(eng, lambda e: e.tensor_copy(out=out, in_=in_), R, W)

    def memset(self, eng, out, val, R=(), W=()):
        return self.P.op(eng, lambda e: e.memset(out, val), R, W)


def mk(K, es, name, shape, dtype=F32):
    t = es.enter_context(K.nc.sbuf_tensor(name, list(shape), dtype))
    return Tl(t, name)


def end_phase(K):
    with K.nc.Block() as block:
        K.P.emit(block, K.sems, K.dsems)


def phase_mod(K):
    nc, O = K.nc, K.O
    with ExitStack() as es:
        ct = mk(K, es, "ct", [128, 8, 3])
        cs = mk(K, es, "cs", [128, 8, 3], BF16)
        wm = [mk(K, es, "wm%d" % i, [128, 8, 512], BF16) for i in range(2)]
        bm = mk(K, es, "bm", [3, 9 * D])
        mo = [mk(K, es, "mo%d" % i, [3, 512]) for i in range(2)]
        O.dma(ct.t[:], K.cT.ap(), W=[ct.b])
        O.dma(bm.t[:], K.b_mod.ap().partition_broadcast(3), W=[bm.b])
        O.act(cs.t[:], ct.t[:], AF.Silu, R=[ct.b], W=[cs.b])
        wv = K.w_mod.ap().rearrange("(k p) c -> p k c", p=128)
        for cb in range(18):
            w = wm[cb % 2]
            ps = K.ps[cb % 2]
            m = mo[cb % 2]
            O.dma(w.t[:], wv[:, :, cb * 512:(cb + 1) * 512], W=[w.b], eng=POOL)
            for k in range(8):
                O.mm(ps.t[0:3, :], cs.t[:, k, :], w.t[:, k, :], k == 0, k == 7, R=[cs.b, w.b], W=[ps.b])
            O.tt(DVE, m.t[:], ps.t[0:3, :], bm.t[:, cb * 512:(cb + 1) * 512], ALU.add, R=[ps.b, bm.b], W=[m.b])
            O.dma(K.MOD.ap()[:, cb * 512:(cb + 1) * 512], m.t[:], R=[m.b], W=[K.bMOD])
        end_phase(K)


def prep_tile(K, O, src_ap, xin, jk, ss, rstd, G, SH, xm, ident, pt, xT, t, src_bufs=()):
    O.dma(xin.t[:], src_ap, R=list(src_bufs), W=[xin.b])
    O.act(jk.t[:], xin.t[:], AF.Square, R=[xin.b], W=[jk.b, ss.b], accum_out=ss.t[:])
    O.ts(DVE, rstd.t[:], ss.t[:], 1.0 / D, ALU.mult, EPS, ALU.add, R=[ss.b], W=[rstd.b])
    O.act(rstd.t[:], rstd.t[:], AF.Sqrt, R=[rstd.b], W=[rstd.b])
    O.recip(rstd.t[:], rstd.t[:], R=[rstd.b], W=[rstd.b])
    O.stt(xm.t[:], xin.t[:], rstd.t[:, 0:1], G.t[:], ALU.mult, ALU.mult, R=[xin.b, rstd.b, G.b], W=[xm.b])
    O.tt(POOL, xm.t[:], xm.t[:], SH.t[:], ALU.add, R=[xm.b, SH.b], W=[xm.b])
    for k in range(8):
        O.tr(pt[k // 4].t[:, (k % 4) * 128:(k % 4 + 1) * 128], xm.t[:, k * 128:(k + 1) * 128], ident.t[:],
             R=[xm.b, ident.b], W=[pt[k // 4].b])
    for h in range(2):
        O.act(xT.t[:, 4 * h:4 * h + 4, t * 128:(t + 1) * 128],
              pt[h].t[:, :].rearrange("p (k t) -> p k t", k=4), AF.Copy, R=[pt[h].b], W=[xT.b])


def load_mod(K, O, ms, mrow, SC, SH, GT, NG):
    m = K.MOD.ap()
    O.dma(SH.t[:], m[ms:ms + 1, mrow * D:(mrow + 1) * D].partition_broadcast(128), R=[K.bMOD], W=[SH.b])
    O.dma(SC.t[:], m[ms:ms + 1, (mrow + 1) * D:(mrow + 2) * D].partition_broadcast(128), R=[K.bMOD], W=[SC.b])
    if isinstance(NG, tuple):
        ngt, norm_d = NG
        O.dma(ngt.t[:], norm_d.ap().partition_broadcast(128), W=[ngt.b])
        NG = ngt
    O.stt(SC.t[:], SC.t[:], 1.0, NG.t[:], ALU.add, ALU.mult, R=[SC.b, NG.b], W=[SC.b])
    if GT is not None:
        O.dma(GT.t[:], m[ms:ms + 1, (mrow + 2) * D:(mrow + 3) * D].partition_broadcast(128), R=[K.bMOD], W=[GT.b])


def phase_ffn(K, tag, w13_d, w2_d, norm_d, mrow, blocks, src_buf, dst_buf):
    nc, O = K.nc, K.O
    with ExitStack() as es:
        w13 = mk(K, es, "w13" + tag, [128, 8, 2 * DFF], BF16)
        w2 = mk(K, es, "w2" + tag, [128, 22, D], BF16)
        SC = mk(K, es, "SC" + tag, [128, D])
        SH = mk(K, es, "SH" + tag, [128, D])
        GT = mk(K, es, "GT" + tag, [128, D])
        xin = [mk(K, es, "xin0" + tag, [128, D])] * 2
        xr = [mk(K, es, "xr%d" % i + tag, [128, D]) for i in range(2)]
        xm = mk(K, es, "xm" + tag, [128, D])
        tmpy = xm
        jk = mk(K, es, "jk" + tag, [128, D], BF16)
        xTs = [mk(K, es, "xT%d" % i + tag, [128, 8, 512], BF16) for i in range(2)]
        gT = mk(K, es, "gT" + tag, [128, 22, 512], BF16)
        sa = [mk(K, es, "sa0" + tag, [128, 512])] * 2
        ss = [mk(K, es, "ss%d" % i + tag, [128, 1]) for i in range(2)]
        rstd = [mk(K, es, "rstd%d" % i + tag, [128, 1]) for i in range(2)]
        ident = K.ident
        w13v = w13_d.ap().rearrange("(k p) c -> p k c", p=128)
        for k in range(8):
            for hh in range(2):
                O.dma(w13.t[:, k, hh * DFF:(hh + 1) * DFF], w13v[:, k, hh * DFF:(hh + 1) * DFF], W=[w13.b], eng=POOL)
        w2v = w2_d.ap().rearrange("(j p) d -> p j d", p=128)
        for j0 in range(0, 22, 2):
            O.dma(w2.t[:, j0:j0 + 2, :], w2v[:, j0:j0 + 2, :], W=[w2.b], eng=POOL)
        NG = (GT, norm_d)
        nt = [0]

        def prep(bi, t):
            ms, tiles = blocks[bi]
            src, dst = tiles[t]
            r = nt[0] % 2
            nt[0] += 1
            prep_tile(K, O, src, xin[r], jk, ss[r], rstd[r], SC, SH, xm, ident, K.ps[0:2], xTs[bi % 2], t,
                      src_bufs=[src_buf])

        cur_ms = None
        for bi, (ms, tiles) in enumerate(blocks):
            xT = xTs[bi % 2]
            same_next = bi + 1 < len(blocks) and blocks[bi + 1][0] == ms
            if ms != cur_ms:
                load_mod(K, O, ms, mrow, SC, SH, GT, NG)
                O.ts(DVE, GT.t[:], GT.t[:], 0.5, ALU.mult, R=[GT.b], W=[GT.b])
                cur_ms = ms
                for t in range(len(tiles)):
                    prep(bi, t)
            for j in range(22):
                pa, pb = K.ps[2 + (j % 2) * 2], K.ps[3 + (j % 2) * 2]
                for k in range(8):
                    O.mm(pa.t[:], w13.t[:, k, j * 128:(j + 1) * 128], xT.t[:, k, :], k == 0, k == 7,
                         R=[w13.b, xT.b], W=[pa.b])
                for k in range(8):
                    O.mm(pb.t[:], w13.t[:, k, DFF + j * 128:DFF + (j + 1) * 128], xT.t[:, k, :], k == 0, k == 7,
                         R=[w13.b, xT.b], W=[pb.b])
                s_ = sa[j % 2]
                O.act(s_.t[:], pa.t[:], AF.Silu, R=[pa.b], W=[s_.b])
                O.tt(DVE, gT.t[:, j, :], s_.t[:], pb.t[:], ALU.mult, R=[s_.b, pb.b], W=[gT.b])
                if same_next and j in (4, 9, 14, 19):
                    prep(bi + 1, (j - 4) // 5)
            for t, (src, dst) in enumerate(tiles):
                xo = xr[t % 2]
                O.dma(xo.t[:], src, R=[src_buf], W=[xo.b])
                for h in range(2):
                    py = K.ps[6 + h]
                    for j in range(22):
                        O.mm(py.t[:], gT.t[:, j, t * 128:(t + 1) * 128], w2.t[:, j, h * 512:(h + 1) * 512],
                             j == 0, j == 21, R=[gT.b, w2.b], W=[py.b])
                    O.tt(DVE, tmpy.t[:, h * 512:(h + 1) * 512], py.t[:], GT.t[:, h * 512:(h + 1) * 512], ALU.mult,
                         R=[py.b, GT.b], W=[tmpy.b])
                O.tt(POOL, xo.t[:], xo.t[:], tmpy.t[:], ALU.add, R=[xo.b, tmpy.b], W=[xo.b])
                O.dma(dst, xo.t[:], R=[xo.b], W=[dst_buf])
        end_phase(K)


def ffn_blocks(src_x, dst_x, src_c=None, dst_c=None):
    blocks = []
    for b in range(NB):
        for blk in range(L // 512):
            tiles = []
            for t in range(4):
                r0 = blk * 512 + t * 128
                tiles.append((src_x.ap()[b, r0:r0 + 128, :], dst_x.ap()[b, r0:r0 + 128, :]))
            blocks.append((b, tiles))
    if src_c is not None:
        tiles = []
        for b in range(NB):
            for t in range(2):
                tiles.append((src_c.ap()[b, t * 128:(t + 1) * 128, :], dst_c.ap()[b, t * 128:(t + 1) * 128, :]))
        blocks.append((2, tiles))
    return blocks


def phase_inproj(K):
    nc, O = K.nc, K.O
    with ExitStack() as es:
        w = mk(K, es, "w_in_s", [128, 8, 5632], BF16)
        NG = mk(K, es, "NGi", [128, D])
        SC = mk(K, es, "SCi", [128, D])
        SH = mk(K, es, "SHi", [128, D])
        xin = [mk(K, es, "xini%d" % i, [128, D]) for i in range(2)]
        xm = mk(K, es, "xmi", [128, D])
        jk = mk(K, es, "jki", [128, D], BF16)
        xT = mk(K, es, "xTi", [128, 8, 512], BF16)
        ss = [mk(K, es, "ssi%d" % i, [128, 1]) for i in range(2)]
        rstd = [mk(K, es, "rstdi%d" % i, [128, 1]) for i in range(2)]
        gq = mk(K, es, "gq", [128, 64])
        gk = mk(K, es, "gk", [128, 64])
        RC = mk(K, es, "RC", [128, 32, 32])
        RS = mk(K, es, "RS", [128, 32, 32])
        raw = [mk(K, es, "rawqk%d" % i, [128, 2048]) for i in range(2)]
        sqb = mk(K, es, "sqb", [128, 2048])
        ss32 = mk(K, es, "ss32", [128, 32])
        mA = mk(K, es, "mA", [128, 1024])
        mB = mk(K, es, "mB", [128, 1024])
        qr = [mk(K, es, "qr%d" % i, [128, 2048]) for i in range(2)]
        QTs = mk(K, es, "QTs", [128, 8, 512], BF16)
        KTs = mk(K, es, "KTs", [128, 8, 512], BF16)
        vs = [mk(K, es, "vs%d" % i, [128, 1024], BF16) for i in range(2)]
        us = [mk(K, es, "us%d" % i, [128, 512]) for i in range(2)]
        gsb = [mk(K, es, "gsb%d" % i, [128, 512]) for i in range(2)]
        ident = K.ident
        wv = K.w_in.ap().rearrange("(k p) c -> p k c", p=128)
        for k in range(8):
            for hh in range(2):
                O.dma(w.t[:, k, hh * DFF:(hh + 1) * DFF], wv[:, k, hh * DFF:(hh + 1) * DFF], W=[w.b], eng=POOL)
        O.dma(NG.t[:], K.norm_mix.ap().partition_broadcast(128), W=[NG.b])
        O.dma(gq.t[:], K.q_norm.ap().partition_broadcast(128), W=[gq.b])
        O.dma(gk.t[:], K.k_norm.ap().partition_broadcast(128), W=[gk.b])
        O.dma(RC.t[:], K.ropec.ap().rearrange("(n p) f -> p n f", p=128), W=[RC.b])
        O.dma(RS.t[:], K.ropes.ap().rearrange("(n p) f -> p n f", p=128), W=[RS.b])
        blocks = []
        for b in range(NB):
            for blk in range(L // 512):
                tl = [(K.X1.ap()[b, blk * 512 + t * 128: blk * 512 + (t + 1) * 128, :], b, blk * 512 + t * 128) for t in range(4)]
                blocks.append((b, True, tl))
        blocks.append((2, False, [(K.C1.ap()[b, t * 128:(t + 1) * 128, :], b, t * 128) for b in range(NB) for t in range(2)]))
        cur_ms = None
        nt = 0
        ucnt = 0
        qcnt = 0
        for (ms, lat, tiles) in blocks:
            if ms != cur_ms:
                load_mod(K, O, ms, 3, SC, SH, None, NG)
                cur_ms = ms
            for t, (src, b, pos) in enumerate(tiles):
                r = nt % 2
                nt += 1
                prep_tile(K, O, src, xin[r], jk, ss[r], rstd[r], SC, SH, xm, ident, K.ps[0:2], xT, t,
                          src_bufs=[K.bX1, K.bC1])
            if lat:
                b = tiles[0][1]
                tok0 = tiles[0][2]
                for m in range(16):
                    pg = K.ps[6 + m % 2]
                    c0 = 3584 + m * 128
                    for k in range(8):
                        O.mm(pg.t[:], w.t[:, k, c0:c0 + 128], xT.t[:, k, :], k == 0, k == 7, R=[w.b, xT.b], W=[pg.b])
                    g = gsb[m % 2]
                    O.act(g.t[:], pg.t[:], AF.Sigmoid, R=[pg.b], W=[g.b])
                    dst = (K.GA if m < 8 else K.GS).ap()[b, (m % 8) * 128:(m % 8 + 1) * 128, tok0:tok0 + 512]
                    O.dma(dst, g.t[:], R=[g.b], W=[K.bG])
            def make_chain(rw, q_r, t, lat, tile_idx):
                def chain():
                    W_ = 2048 if lat else 1024
                    G_ = W_ // 64
                    g3 = lambda tl_: tl_.t[:, 0:W_].rearrange("p (g d) -> p g d", d=64)
                    O.act(sqb.t[:, 0:W_], rw.t[:, 0:W_], AF.Square, R=[rw.b], W=[sqb.b])
                    O.red(ss32.t[:, 0:G_], g3(sqb), ALU.add, R=[sqb.b], W=[ss32.b])
                    O.ts(DVE, ss32.t[:, 0:G_], ss32.t[:, 0:G_], 1.0 / 64, ALU.mult, EPS, ALU.add, R=[ss32.b], W=[ss32.b])
                    O.act(ss32.t[:, 0:G_], ss32.t[:, 0:G_], AF.Sqrt, R=[ss32.b], W=[ss32.b])
                    O.recip(ss32.t[:, 0:G_], ss32.t[:, 0:G_], R=[ss32.b], W=[ss32.b])
                    O.tt(DVE, g3(sqb), g3(rw), ss32.t[:, 0:G_].unsqueeze(2).to_broadcast([128, G_, 64]), ALU.mult,
                         R=[rw.b, ss32.b, sqb.b], W=[sqb.b])
                    gqb = gq.t[:, :].unsqueeze(1).to_broadcast([128, 16, 64])
                    gkb = gk.t[:, :].unsqueeze(1).to_broadcast([128, 16, 64])
                    if lat:
                        O.tt(DVE, g3(sqb)[:, 0:16, :], g3(sqb)[:, 0:16, :], gqb, ALU.mult, R=[sqb.b, gq.b], W=[sqb.b])
                        O.tt(DVE, g3(sqb)[:, 16:32, :], g3(sqb)[:, 16:32, :], gkb, ALU.mult, R=[sqb.b, gk.b], W=[sqb.b])
                        v4 = sqb.t[:, 0:W_].rearrange("p (g h d) -> p g h d", h=2, d=32)
                        o4 = q_r.t[:, 0:W_].rearrange("p (g h d) -> p g h d", h=2, d=32)
                        t1, t2 = v4[:, :, 0, :], v4[:, :, 1, :]
                        cosb = RC.t[:, tile_idx:tile_idx + 1, :].to_broadcast([128, G_, 32])
                        sinb = RS.t[:, tile_idx:tile_idx + 1, :].to_broadcast([128, G_, 32])
                        mAv = mA.t[:, 0:G_ * 32].rearrange("p (g d) -> p g d", d=32)
                        mBv = mB.t[:, 0:G_ * 32].rearrange("p (g d) -> p g d", d=32)
                        O.tt(DVE, mAv, t1, cosb, ALU.mult, R=[sqb.b, RC.b], W=[mA.b])
                        O.tt(DVE, mBv, t2, sinb, ALU.mult, R=[sqb.b, RS.b], W=[mB.b])
                        O.tt(DVE, o4[:, :, 0, :], mAv, mBv, ALU.subtract, R=[mA.b, mB.b], W=[q_r.b])
                        O.tt(DVE, mAv, t2, cosb, ALU.mult, R=[sqb.b, RC.b], W=[mA.b])
                        O.tt(DVE, mBv, t1, sinb, ALU.mult, R=[sqb.b, RS.b], W=[mB.b])
                        O.tt(DVE, o4[:, :, 1, :], mAv, mBv, ALU.add, R=[mA.b, mB.b, q_r.b], W=[q_r.b])
                    else:
                        O.tt(DVE, g3(q_r), g3(sqb), gkb, ALU.mult, R=[sqb.b, gk.b], W=[q_r.b])

                def post():
                    W_ = 2048 if lat else 1024
                    for un in range(W_ // 512):
                        n = un if lat else un + 2
                        pt = K.ps[un % 2]
                        for i in range(4):
                            c0 = un * 512 + i * 128
                            O.tr(pt.t[:, i * 128:(i + 1) * 128], q_r.t[:, c0:c0 + 128], ident.t[:], R=[q_r.b, ident.b], W=[pt.b])
                        dT = QTs if n < 2 else KTs
                        hh = (n % 2) * 4
                        O.act(dT.t[:, hh:hh + 4, t * 128:(t + 1) * 128], pt.t[:, :].rearrange("p (k t) -> p k t", k=4),
                              AF.Copy, R=[pt.b], W=[dT.b])
                return chain, post

            pending = None
            for t, (src, b, pos) in enumerate(tiles):
                kpos = pos + LC if lat else pos
                tile_idx = pos // 128
                rw, q_r = raw[qcnt % 2], qr[qcnt % 2]
                qcnt += 1
                units = list(range(7)) if lat else list(range(2, 7))
                for n in units:
                    pb = K.ps[2 + ucnt % 4]
                    ucnt += 1
                    for k in range(8):
                        O.mm(pb.t[:], xT.t[:, k, t * 128:(t + 1) * 128], w.t[:, k, n * 512:(n + 1) * 512], k == 0, k == 7,
                             R=[xT.b, w.b], W=[pb.b])
                    if n < 4:
                        col = (n if lat else n - 2) * 512
                        O.act(rw.t[:, col:col + 512], pb.t[:], AF.Copy, R=[pb.b], W=[rw.b])
                    elif n < 6:
                        v_ = vs[(nt + t) % 2]
                        O.copy(DVE, v_.t[:, (n - 4) * 512:(n - 3) * 512], pb.t[:], R=[pb.b], W=[v_.b])
                        if n == 5:
                            O.dma(K.V.ap()[b, kpos:kpos + 128, :], v_.t[:], R=[v_.b], W=[K.bV])
                    else:
                        u_ = us[(nt + t) % 2]
                        O.copy(DVE, u_.t[:], pb.t[:], R=[pb.b], W=[u_.b])
                        O.dma(K.U.ap()[b, kpos:kpos + 128, :], u_.t[:], R=[u_.b], W=[K.bU])
                if pending is not None:
                    pending[0]()
                    pending[1]()
                pending = make_chain(rw, q_r, t, lat, tile_idx)
            pending[0]()
            pending[1]()
            if lat:
                b = tiles[0][1]
                tok0 = tiles[0][2]
                O.dma(K.QT.ap()[b].rearrange("(h p) t -> p h t", p=128)[:, :, tok0:tok0 + 512], QTs.t[:], R=[QTs.b], W=[K.bQT])
                O.dma(K.KT.ap()[b].rearrange("(h p) t -> p h t", p=128)[:, :, LC + tok0:LC + tok0 + 512], KTs.t[:], R=[KTs.b], W=[K.bKT])
            else:
                for b in range(NB):
                    O.dma(K.KT.ap()[b].rearrange("(h p) t -> p h t", p=128)[:, :, 0:LC], KTs.t[:, :, b * 256:(b + 1) * 256],
                          R=[KTs.b], W=[K.bKT])
        end_phase(K)


def phase_attn(K):
    nc, O = K.nc, K.O
    with ExitStack() as es:
        KTh = [mk(K, es, "KTh%d" % i, [128, LK], BF16) for i in range(2)]
        Qm = [[mk(K, es, "Qm%d_%d" % (i, c), [128, L], BF16) for c in range(2)] for i in range(2)]
        for i in range(2):
            for c in range(2):
                O.memset(POOL, Qm[i][c].t[:], 0.0, W=[Qm[i][c].b])
        Vh = [mk(K, es, "Vh%d" % i, [128, 34, 128], BF16) for i in range(2)]
        Et = [mk(K, es, "Et%d" % i, [128, 512], BF16) for i in range(4)]
        ones_b = mk(K, es, "ones_b", [128, 128], BF16)
        ones_f = mk(K, es, "ones_f", [128, 128])
        lv = mk(K, es, "lv", [128, 256])
        lp = mk(K, es, "lp", [128, 2, 64])
        ls = mk(K, es, "ls", [128, 2])
        nlam = mk(K, es, "nlam", [128, 1])
        subc = mk(K, es, "subc", [128, 1])
        rc = mk(K, es, "rc", [128, 512])
        o0 = mk(K, es, "o0", [128, 512])
        o1 = mk(K, es, "o1", [128, 512])
        osq = mk(K, es, "osq", [128, 512])
        rs = mk(K, es, "rs", [128, 512])
        OTs = [mk(K, es, "OTs%d" % i, [128, 512], BF16) for i in range(2)]
        O.memset(POOL, ones_b.t[:], 1.0, W=[ones_b.b])
        O.memset(POOL, ones_f.t[:], 1.0, W=[ones_f.b])
        O.dma(lv.t[:], K.lamv.ap().partition_broadcast(128), W=[lv.b])
        O.dma(subc.t[:], K.subln.ap(), W=[subc.b])
        lvv = lv.t[:, :].rearrange("p (a b d) -> p a b d", a=2, b=2)
        O.tt(DVE, lp.t[:], lvv[:, :, 0, :], lvv[:, :, 1, :], ALU.mult, R=[lv.b], W=[lp.b])
        O.red(ls.t[:], lp.t[:], ALU.add, R=[lp.b], W=[ls.b])
        O.act(ls.t[:], ls.t[:], AF.Exp, R=[ls.b], W=[ls.b])
        O.tt(DVE, nlam.t[:], ls.t[:, 1:2], ls.t[:, 0:1], ALU.subtract, R=[ls.b], W=[nlam.b])
        O.ts(DVE, nlam.t[:], nlam.t[:], -LAM_INIT, ALU.add, R=[nlam.b], W=[nlam.b])
        O.ts(DVE, subc.t[:], subc.t[:], 1.0 - LAM_INIT, ALU.mult, R=[subc.b], W=[subc.b])
        psc = K.ps[0:3]
        pO = K.ps[3:5]
        pS = K.ps[5:7]
        pss = K.ps[7]
        hidx = 0
        for b in range(NB):
            for h in range(8):
                kt, qm, vh = KTh[hidx % 2], Qm[hidx % 2], Vh[hidx % 2]
                hidx += 1
                O.dma(kt.t[:], K.KT.ap()[b, h * 128:(h + 1) * 128, :], R=[K.bKT], W=[kt.b])
                for c in range(2):
                    O.dma(qm[c].t[c * 64:(c + 1) * 64, :], K.QT.ap()[b, h * 128 + c * 64:h * 128 + (c + 1) * 64, :],
                          R=[K.bQT], W=[qm[c].b])
                O.dma(vh.t[:], K.V.ap()[b].rearrange("(n p) e -> p n e", p=128)[:, :, h * 128:(h + 1) * 128], R=[K.bV], W=[vh.b])
                seq = [(qb, c, kc) for qb in range(8) for c in range(2) for kc in range(34)]

                def sc(i):
                    qb, c, kc = seq[i]
                    p = psc[i % 3]
                    O.mm(p.t[:], kt.t[:, kc * 128:(kc + 1) * 128],
                         qm[c].t[:, qb * 512:(qb + 1) * 512], True, True, R=[kt.b, qm[c].b], W=[p.b])

                def ex(i):
                    p = psc[i % 3]
                    e_ = Et[i % 4]
                    O.act(e_.t[:], p.t[:], AF.Exp, R=[p.b], W=[e_.b], scale=0.125)

                def av(i):
                    qb, c, kc = seq[i]
                    e_ = Et[i % 4]
                    O.mm(pO[c].t[:], vh.t[:, kc, :], e_.t[:], kc == 0, kc == 33, R=[vh.b, e_.b], W=[pO[c].b])
                    O.mm(pS[c].t[:], ones_b.t[:], e_.t[:], kc == 0, kc == 33, R=[ones_b.b, e_.b], W=[pS[c].b])

                def epi2(qb):
                    ot = OTs[qb % 2]
                    O.mm(pss.t[:], ones_f.t[:], osq.t[:], True, True, R=[ones_f.b, osq.b], W=[pss.b])
                    O.ts(DVE, rs.t[:], pss.t[:], 1.0 / 128, ALU.mult, EPS, ALU.add, R=[pss.b], W=[rs.b])
                    O.act(rs.t[:], rs.t[:], AF.Ln, R=[rs.b], W=[rs.b])
                    O.act(rs.t[:], rs.t[:], AF.Exp, R=[rs.b], W=[rs.b], scale=-0.5)
                    O.stt(ot.t[:], o0.t[:], subc.t[:, 0:1], rs.t[:], ALU.mult, ALU.mult, R=[o0.b, subc.b, rs.b], W=[ot.b])
                    O.dma(K.OT.ap()[b, h * 128:(h + 1) * 128, qb * 512:(qb + 1) * 512], ot.t[:], R=[ot.b], W=[K.bOT])

                deferred = {}
                sc(0)
                sc(1)
                for i in range(len(seq)):
                    ex(i)
                    if i + 2 < len(seq):
                        sc(i + 2)
                    av(i)
                    if i in deferred:
                        epi2(deferred.pop(i))
                    qb, c, kc = seq[i]
                    if c == 1 and kc == 33:
                        if i + 10 < len(seq):
                            deferred[i + 10] = qb
                        ot = OTs[qb % 2]
                        O.recip(rc.t[:], pS[0].t[:], R=[pS[0].b], W=[rc.b])
                        O.tt(DVE, o0.t[:], pO[0].t[:], rc.t[:], ALU.mult, R=[pO[0].b, rc.b], W=[o0.b])
                        O.recip(rc.t[:], pS[1].t[:], R=[pS[1].b], W=[rc.b])
                        O.tt(DVE, o1.t[:], pO[1].t[:], rc.t[:], ALU.mult, R=[pO[1].b, rc.b], W=[o1.b])
                        O.stt(o0.t[:], o1.t[:], nlam.t[:, 0:1], o0.t[:], ALU.mult, ALU.add, R=[o1.b, o0.b, nlam.b], W=[o0.b])
                        O.tt(POOL, osq.t[:], o0.t[:], o0.t[:], ALU.mult, R=[o0.b], W=[osq.b])
                        if i + 10 >= len(seq):
                            epi2(qb)
        end_phase(K)


TWO_PI = 2.0 * math.pi
CW1 = 6.28125
CW2 = TWO_PI - CW1
NCH = LK // 8
NCC = LC // 8


def sincos(K, O, A, S_, C_, tmps, itile, shape_ap):
    kf, r, m = tmps
    v = shape_ap
    O.ts(DVE, v(kf), v(A), 1.0 / TWO_PI, ALU.mult, R=[A.b], W=[kf.b])
    O.copy(DVE, v(itile), v(kf), R=[kf.b], W=[itile.b])
    O.copy(DVE, v(kf), v(itile), R=[itile.b], W=[kf.b])
    O.stt(v(r), v(kf), -CW1, v(A), ALU.mult, ALU.add, R=[kf.b, A.b], W=[r.b])
    O.stt(v(r), v(kf), -CW2, v(r), ALU.mult, ALU.add, R=[kf.b, r.b], W=[r.b])
    shp = list(v(r).shape)
    pib = K.picol.t[:, 0:1].to_broadcast(shp)
    npib = K.picol.t[:, 1:2].to_broadcast(shp)
    O.tt(DVE, v(m), v(r), pib, ALU.is_gt, R=[r.b, K.picol.b], W=[m.b])
    O.stt(v(r), v(m), -TWO_PI, v(r), ALU.mult, ALU.add, R=[m.b, r.b], W=[r.b])
    O.tt(DVE, v(m), v(r), npib, ALU.is_lt, R=[r.b, K.picol.b], W=[m.b])
    O.stt(v(r), v(m), TWO_PI, v(r), ALU.mult, ALU.add, R=[m.b, r.b], W=[r.b])
    O.ts(DVE, v(r), v(r), 3.141592, ALU.min, -3.141592, ALU.max, R=[r.b], W=[r.b])
    O.act(v(S_), v(r), AF.Sin, R=[r.b], W=[S_.b])
    O.ts(DVE, v(m), v(r), -1.0, ALU.mult, R=[r.b], W=[m.b])
    O.tt(DVE, v(m), v(m), v(r), ALU.max, R=[m.b, r.b], W=[m.b])
    O.ts(DVE, v(m), v(m), -1.0, ALU.mult, math.pi / 2, ALU.add, R=[m.b], W=[m.b])
    O.act(v(C_), v(m), AF.Sin, R=[m.b], W=[C_.b])


def phase_s5(K):
    nc, O = K.nc, K.O
    with ExitStack() as esP:
        Wbm = [[mk(K, esP, "Wbm%d%d" % (ri, d), [128, 32, 128], BF16) for d in range(2)] for ri in range(2)]
        Wcm = [[mk(K, esP, "Wcm%d%d" % (d, ri), [128, 32, 128], BF16) for ri in range(2)] for d in range(2)]
        M = [mk(K, esP, "M%d" % d, [128, 32, 128], BF16) for d in range(2)]
        r8 = mk(K, esP, "r8", [128, 32])
        ph8 = mk(K, esP, "ph8", [128, 32])
        nvec = mk(K, esP, "nvec", [128, NCH])
        O.dma(nvec.t[:], K.nvecd.ap(), W=[nvec.b])
        K.picol = mk(K, esP, "picol", [128, 2])
        O.memset(DVE, K.picol.t[:, 0:1], math.pi, W=[K.picol.b])
        O.memset(DVE, K.picol.t[:, 1:2], -math.pi, W=[K.picol.b])
        ident = K.ident
        with ExitStack() as es:
            sp = mk(K, es, "s5p_s", [128, 3, 32])
            Bt = mk(K, es, "s5B_s", [128, 2, 32, 16])
            Ct = mk(K, es, "s5C_s", [128, 2, 32, 16])
            CM = mk(K, es, "CM_s", [128, 2, 128])
            O.dma(sp.t[:], K.s5p.ap(), W=[sp.b])
            O.dma(Bt.t[:], K.s5B.ap(), W=[Bt.b])
            O.dma(Ct.t[:], K.s5C.ap(), W=[Ct.b])
            O.dma(CM.t[:], K.CMd.ap(), W=[CM.b])
            sm = {n: mk(K, es, "sm_" + n, [128, 32]) for n in
                  "dt x th mag s c ar ai t1 t2 den nr fr fi kf r m th8".split()}
            smi = es.enter_context(nc.sbuf_tensor("smi", [128, 32], mybir.dt.int32))
            smi = Tl(smi, "smi")
            PWR = mk(K, es, "PWR", [128, 9, 32])
            PWI = mk(K, es, "PWI", [128, 9, 32])
            NPR = mk(K, es, "NPR", [128, 9, 32])
            NPI = mk(K, es, "NPI", [128, 9, 32])
            pt1 = mk(K, es, "pt1", [128, 8, 32])
            pt2 = mk(K, es, "pt2", [128, 8, 32])
            lre, lim, ldt = sp.t[:, 0, :], sp.t[:, 1, :], sp.t[:, 2, :]
            A = lambda n: sm[n].t[:]
            Bf = lambda n: sm[n].b
            O.act(A("dt"), ldt, AF.Exp, R=[sp.b], W=[Bf("dt")])
            O.tt(DVE, A("x"), lre, A("dt"), ALU.mult, R=[sp.b, Bf("dt")], W=[Bf("x")])
            O.tt(DVE, A("th"), lim, A("dt"), ALU.mult, R=[sp.b, Bf("dt")], W=[Bf("th")])
            O.act(A("mag"), A("x"), AF.Exp, R=[Bf("x")], W=[Bf("mag")])
            O.act(r8.t[:], A("x"), AF.Exp, R=[Bf("x")], W=[r8.b], scale=8.0)
            sincos(K, O, sm["th"], sm["s"], sm["c"], (sm["kf"], sm["r"], sm["m"]), smi, lambda tl: tl.t[:])
            O.tt(DVE, A("ar"), A("mag"), A("c"), ALU.mult, R=[Bf("mag"), Bf("c")], W=[Bf("ar")])
            O.tt(DVE, A("ai"), A("mag"), A("s"), ALU.mult, R=[Bf("mag"), Bf("s")], W=[Bf("ai")])
            O.ts(DVE, A("th8"), A("th"), 8.0, ALU.mult, R=[Bf("th")], W=[Bf("th8")])
            O.ts(DVE, A("kf"), A("th8"), 1.0 / TWO_PI, ALU.mult, R=[Bf("th8")], W=[Bf("kf")])
            O.copy(DVE, smi.t[:], A("kf"), R=[Bf("kf")], W=[smi.b])
            O.copy(DVE, A("kf"), smi.t[:], R=[smi.b], W=[Bf("kf")])
            O.stt(ph8.t[:], A("kf"), -CW1, A("th8"), ALU.mult, ALU.add, R=[Bf("kf"), Bf("th8")], W=[ph8.b])
            O.stt(ph8.t[:], A("kf"), -CW2, ph8.t[:], ALU.mult, ALU.add, R=[Bf("kf"), ph8.b], W=[ph8.b])
            O.tt(DVE, A("t1"), lre, lre, ALU.mult, R=[sp.b], W=[Bf("t1")])
            O.tt(DVE, A("t2"), lim, lim, ALU.mult, R=[sp.b], W=[Bf("t2")])
            O.tt(DVE, A("den"), A("t1"), A("t2"), ALU.add, R=[Bf("t1"), Bf("t2")], W=[Bf("den")])
            O.recip(A("den"), A("den"), R=[Bf("den")], W=[Bf("den")])
            O.ts(DVE, A("nr"), A("ar"), -1.0, ALU.add, R=[Bf("ar")], W=[Bf("nr")])
            O.tt(DVE, A("t1"), A("nr"), lre, ALU.mult, R=[Bf("nr"), sp.b], W=[Bf("t1")])
            O.tt(DVE, A("t2"), A("ai"), lim, ALU.mult, R=[Bf("ai"), sp.b], W=[Bf("t2")])
            O.tt(DVE, A("fr"), A("t1"), A("t2"), ALU.add, R=[Bf("t1"), Bf("t2")], W=[Bf("fr")])
            O.tt(DVE, A("fr"), A("fr"), A("den"), ALU.mult, R=[Bf("fr"), Bf("den")], W=[Bf("fr")])
            O.tt(DVE, A("t1"), A("ai"), lre, ALU.mult, R=[Bf("ai"), sp.b], W=[Bf("t1")])
            O.tt(DVE, A("t2"), A("nr"), lim, ALU.mult, R=[Bf("nr"), sp.b], W=[Bf("t2")])
            O.tt(DVE, A("fi"), A("t1"), A("t2"), ALU.subtract, R=[Bf("t1"), Bf("t2")], W=[Bf("fi")])
            O.tt(DVE, A("fi"), A("fi"), A("den"), ALU.mult, R=[Bf("fi"), Bf("den")], W=[Bf("fi")])
            O.memset(DVE, PWR.t[:, 0, :], 1.0, W=[PWR.b])
            O.memset(DVE, PWI.t[:, 0, :], 0.0, W=[PWI.b])
            O.copy(DVE, PWR.t[:, 1, :], A("ar"), R=[Bf("ar")], W=[PWR.b])
            O.copy(DVE, PWI.t[:, 1, :], A("ai"), R=[Bf("ai")], W=[PWI.b])
            for k in range(2, 9):
                O.tt(DVE, A("t1"), PWR.t[:, k - 1, :], A("ar"), ALU.mult, R=[PWR.b, Bf("ar")], W=[Bf("t1")])
                O.tt(DVE, A("t2"), PWI.t[:, k - 1, :], A("ai"), ALU.mult, R=[PWI.b, Bf("ai")], W=[Bf("t2")])
                O.tt(DVE, PWR.t[:, k, :], A("t1"), A("t2"), ALU.subtract, R=[Bf("t1"), Bf("t2")], W=[PWR.b])
                O.tt(DVE, A("t1"), PWR.t[:, k - 1, :], A("ai"), ALU.mult, R=[PWR.b, Bf("ai")], W=[Bf("t1")])
                O.tt(DVE, A("t2"), PWI.t[:, k - 1, :], A("ar"), ALU.mult, R=[PWI.b, Bf("ar")], W=[Bf("t2")])
                O.tt(DVE, PWI.t[:, k, :], A("t1"), A("t2"), ALU.add, R=[Bf("t1"), Bf("t2")], W=[PWI.b])
            O.tt(DVE, pt1.t[:], PWR.t[:, 1:9, :], PWR.t[:, 1:9, :], ALU.mult, R=[PWR.b], W=[pt1.b])
            O.tt(DVE, pt2.t[:], PWI.t[:, 1:9, :], PWI.t[:, 1:9, :], ALU.mult, R=[PWI.b], W=[pt2.b])
            O.tt(DVE, pt1.t[:], pt1.t[:], pt2.t[:], ALU.add, R=[pt1.b, pt2.b], W=[pt1.b])
            O.recip(pt1.t[:], pt1.t[:], R=[pt1.b], W=[pt1.b])
            O.tt(DVE, NPR.t[:, 1:9, :], PWR.t[:, 1:9, :], pt1.t[:], ALU.mult, R=[PWR.b, pt1.b], W=[NPR.b])
            O.tt(DVE, pt2.t[:], PWI.t[:, 1:9, :], pt1.t[:], ALU.mult, R=[PWI.b, pt1.b], W=[pt2.b])
            O.ts(DVE, NPI.t[:, 1:9, :], pt2.t[:], -1.0, ALU.mult, R=[pt2.b], W=[NPI.b])
            PB = [mk(K, es, "PB%d" % i, [128, 32, 8]) for i in range(2)]
            PNB = [mk(K, es, "PNB%d" % i, [128, 32, 8]) for i in range(2)]
            PC = [mk(K, es, "PC%d" % i, [128, 32, 8]) for i in range(2)]
            for j in range(8):
                for (dst, srcs, k0, k1) in ((PB, (PWR, PWI), 7 - j, j), (PNB, (NPR, NPI), 1 + j, 8 - j), (PC, (PWR, PWI), j + 1, 8 - j)):
                    for ri in range(2):
                        O.copy(POOL, dst[ri].t[0:64, :, j], srcs[ri].t[0:64, k0, :], R=[srcs[ri].b], W=[dst[ri].b])
                        O.copy(POOL, dst[ri].t[64:128, :, j], srcs[ri].t[64:128, k1, :], R=[srcs[ri].b], W=[dst[ri].b])
            Bb = [mk(K, es, "Bb%d" % i, [128, 32, 16]) for i in range(2)]
            q1 = mk(K, es, "q1", [128, 32, 16])
            q2 = mk(K, es, "q2", [128, 32, 16])
            frb = sm["fr"].t[:, :].unsqueeze(2).to_broadcast([128, 32, 16])
            fib = sm["fi"].t[:, :].unsqueeze(2).to_broadcast([128, 32, 16])
            Br, Bi = Bt.t[:, 0, :, :], Bt.t[:, 1, :, :]
            O.tt(DVE, q1.t[:], Br, frb, ALU.mult, R=[Bt.b, Bf("fr")], W=[q1.b])
            O.tt(DVE, q2.t[:], Bi, fib, ALU.mult, R=[Bt.b, Bf("fi")], W=[q2.b])
            O.tt(DVE, Bb[0].t[:], q1.t[:], q2.t[:], ALU.subtract, R=[q1.b, q2.b], W=[Bb[0].b])
            O.tt(DVE, q1.t[:], Bi, frb, ALU.mult, R=[Bt.b, Bf("fr")], W=[q1.b])
            O.tt(DVE, q2.t[:], Br, fib, ALU.mult, R=[Bt.b, Bf("fi")], W=[q2.b])
            O.tt(DVE, Bb[1].t[:], q1.t[:], q2.t[:], ALU.add, R=[q1.b, q2.b], W=[Bb[1].b])
            T1 = mk(K, es, "T1", [128, 32, 128])
            T2 = mk(K, es, "T2", [128, 32, 128])
            T3 = mk(K, es, "T3", [128, 32, 128])
            T4 = mk(K, es, "T4", [128, 32, 128])
            T3b = mk(K, es, "T3b", [128, 32, 128], BF16)
            T4b = mk(K, es, "T4b", [128, 32, 128], BF16)

            def v4(tl):
                return tl.t[:, :, :].rearrange("p g (j c) -> p g j c", j=8)

            def cmul_outer(Pw, Xr, Xi, Xb, outRe, outIm, neg_im):
                pr = Pw[0].t[:, :, :].unsqueeze(3).to_broadcast([128, 32, 8, 16])
                pi_ = Pw[1].t[:, :, :].unsqueeze(3).to_broadcast([128, 32, 8, 16])
                xr = Xr.unsqueeze(2).to_broadcast([128, 32, 8, 16])
                xi = Xi.unsqueeze(2).to_broadcast([128, 32, 8, 16])
                O.tt(DVE, v4(T1), pr, xr, ALU.mult, R=[Pw[0].b] + Xb, W=[T1.b])
                O.tt(DVE, v4(T2), pi_, xi, ALU.mult, R=[Pw[1].b] + Xb, W=[T2.b])
                O.tt(POOL, outRe, T1.t[:], T2.t[:], ALU.subtract, R=[T1.b, T2.b], W=[outRe_b[0]])
                O.tt(DVE, v4(T1), pr, xi, ALU.mult, R=[Pw[0].b] + Xb, W=[T1.b])
                O.tt(DVE, v4(T2), pi_, xr, ALU.mult, R=[Pw[1].b] + Xb, W=[T2.b])
                if neg_im:
                    O.stt(outIm, T1.t[:], -1.0, T2.t[:], ALU.mult, ALU.subtract, R=[T1.b, T2.b], W=[outIm_b[0]])
                else:
                    O.tt(POOL, outIm, T1.t[:], T2.t[:], ALU.add, R=[T1.b, T2.b], W=[outIm_b[0]])

            outRe_b, outIm_b = [T3.b], [T4.b]
            cmul_outer(PC, Ct.t[:, 0, :, :], Ct.t[:, 1, :, :], [Ct.b], T3.t[:], T4.t[:], True)
            for d in range(2):
                hs = slice(d * 64, (d + 1) * 64)
                for ri, src in enumerate((T3, T4)):
                    O.memset(POOL, Wcm[d][ri].t[:], 0.0, W=[Wcm[d][ri].b])
                    O.act(Wcm[d][ri].t[hs], src.t[hs], AF.Copy, R=[src.b], W=[Wcm[d][ri].b])
            for ri in range(2):
                for d in range(2):
                    O.memset(POOL, Wbm[ri][d].t[:], 0.0, W=[Wbm[ri][d].b])
            outRe_b, outIm_b = [T3.b], [T4.b]
            cmul_outer(PB, Bb[0].t[:], Bb[1].t[:], [Bb[0].b, Bb[1].b], T3.t[:], T4.t[:], False)
            cnt = 0
            for ri, src in enumerate((T3, T4)):
                for g0 in range(0, 32, 4):
                    pb = K.ps[cnt % 2]
                    cnt += 1
                    for gg in range(4):
                        O.tr(pb.t[:, gg * 128:(gg + 1) * 128], src.t[:, g0 + gg, :], ident.t[:], R=[src.b, ident.b], W=[pb.b])
                    pv = pb.t[:, :].rearrange("p (g x) -> p g x", g=4)
                    for d in range(2):
                        O.act(Wbm[ri][d].t[:, g0:g0 + 4, d * 64:(d + 1) * 64], pv[:, :, d * 64:(d + 1) * 64], AF.Copy,
                              R=[pb.b], W=[Wbm[ri][d].b])
            outRe_b, outIm_b = [T3b.b], [T4b.b]
            cmul_outer(PNB, Bb[0].t[:], Bb[1].t[:], [Bb[0].b, Bb[1].b], T3b.t[:], T4b.t[:], False)
            cnt = 0
            for d in range(2):
                for g0 in range(0, 32, 4):
                    pb = K.ps[2 + cnt % 2]
                    cnt += 1
                    for gg in range(4):
                        g = g0 + gg
                        O.mm(pb.t[:, gg * 128:(gg + 1) * 128], T3b.t[:, g, :], Wcm[d][0].t[:, g, :], True, False,
                             R=[T3b.b, Wcm[d][0].b], W=[pb.b])
                        O.mm(pb.t[:, gg * 128:(gg + 1) * 128], T4b.t[:, g, :], Wcm[d][1].t[:, g, :], False, True,
                             R=[T4b.b, Wcm[d][1].b], W=[pb.b])
                    O.tt(DVE, M[d].t[:, g0:g0 + 4, :], pb.t[:, :].rearrange("p (g x) -> p g x", g=4),
                         CM.t[:, d:d + 1, :].to_broadcast([128, 4, 128]), ALU.mult, R=[pb.b, CM.b], W=[M[d].b])
            end_phase(K)
        with ExitStack() as es:
            angt = mk(K, es, "angt", [128, NCH])
            kft = mk(K, es, "kft", [128, NCH]); rrt = mk(K, es, "rrt", [128, NCH]); mmt = mk(K, es, "mmt", [128, NCH])
            kit = Tl(es.enter_context(nc.sbuf_tensor("kit", [128, NCH], mybir.dt.int32)), "kit")
            sTt = [mk(K, es, "sTt%d" % i, [128, NCH]) for i in range(2)]
            cTt = [mk(K, es, "cTt%d" % i, [128, NCH]) for i in range(2)]
            for g in range(32):
                s_, c_ = sTt[g % 2], cTt[g % 2]
                O.ts(DVE, angt.t[:], nvec.t[:], ph8.t[:, g:g + 1], ALU.mult, R=[nvec.b, ph8.b], W=[angt.b])
                sincos(K, O, angt, s_, c_, (kft, rrt, mmt), kit, lambda tl: tl.t[:])
                O.dma(K.TBL.ap()[g, 0], s_.t[:], R=[s_.b], W=[K.bTBL])
                O.dma(K.TBL.ap()[g, 1], c_.t[:], R=[c_.b], W=[K.bTBL])
            end_phase(K)
        with ExitStack() as es:
            nb_b = NB if "b" in K.s5_parts else 0
            Ulr = [mk(K, es, "Ulr%d" % m, [128, 8, 128]) for m in range(2)]
            Ul = [mk(K, es, "Ul%d" % m, [128, 8, 128]) for m in range(4)]
            Uc = mk(K, es, "Uc", [128, 8, 128])
            O.memset(POOL, Uc.t[:], 0.0, W=[Uc.b])
            Yt = [mk(K, es, "Yt%d" % m, [128, 8, 128]) for m in range(4)]
            gout = Ulr
            perm_in = lambda tl: tl.t[:, :, :].rearrange("p i (g c) -> p g i c", g=8)
            v_gic = lambda tl: tl.t[:, :, :].rearrange("p g (i c) -> p g i c", i=8)
            Dbc = mk(K, es, "Dbc", [128, 512])
            Ugf = [mk(K, es, "Ugf%d" % i, [128, NCH], BF16) for i in range(2)]
            Ugr = [mk(K, es, "Ugr%d" % i, [128, NCH], BF16) for i in range(2)]
            Ssbs = [mk(K, es, "Ssb%d" % i, [128, 2, NCH]) for i in range(2)]
            Zs = [mk(K, es, "Z%d" % i, [128, 2, NCH]) for i in range(2)]
            Qs = [mk(K, es, "Q%d" % i, [128, 2, NCH]) for i in range(2)]
            Hs = [mk(K, es, "H%d" % i, [128, 2, NCH], BF16) for i in range(2)]
            a1 = mk(K, es, "a1", [128, NCH]); a2 = mk(K, es, "a2", [128, NCH])
            a3 = mk(K, es, "a3", [128, NCH]); a4 = mk(K, es, "a4", [128, NCH])
            cTs = [mk(K, es, "cTb%d" % i, [128, NCH]) for i in range(2)]
            sTs = [mk(K, es, "sTb%d" % i, [128, NCH]) for i in range(2)]
            Yr = mk(K, es, "Yr", [128, 512])
            Yg = mk(K, es, "Yg", [128, 512])
            g1 = mk(K, es, "g1", [128, 8, 128]); g2 = mk(K, es, "g2", [128, 8, 128])
            O.dma(Dbc.t[:], K.s5D.ap().partition_broadcast(128), W=[Dbc.b])
            pA, pB_, SL0, SL1, SCb, PY0, PY1, TO = K.ps
            SL = [SL0, SL1]
            PY = [PY0, PY1]
            it = 0
            for b in range(nb_b):
                for qtr in range(4):
                    if K.s5_limit is not None and (b, qtr) != (0, 0):
                        continue
                    cs = slice(qtr * 128, (qtr + 1) * 128)
                    for m in range(5):
                        raw = Ulr[m % 2]
                        if m < 4:
                            O.dma(raw.t[:], K.U.ap()[b, LC + 1024 * m:LC + 1024 * (m + 1), cs].rearrange("(n i) c -> n i c", i=8),
                                  R=[K.bU], W=[raw.b])
                            O.act(v_gic(Ul[m]), perm_in(raw), AF.Copy, R=[raw.b], W=[Ul[m].b])
                        else:
                            O.dma(raw.t[0:32], K.U.ap()[b, 0:LC, cs].rearrange("(n i) c -> n i c", i=8), R=[K.bU], W=[raw.b])
                            O.act(Uc.t[0:32, :, :].rearrange("p g (i c) -> p g i c", i=8),
                                  raw.t[0:32, :, :].rearrange("p i (g c) -> p g i c", g=8), AF.Copy, R=[raw.b], W=[Uc.b])
                    for gl in range(8):
                        g = qtr * 8 + gl
                        uf, ur = Ugf[it % 2], Ugr[it % 2]
                        Ssb, Z, Q, H, cT, sT = Ssbs[it % 2], Zs[it % 2], Qs[it % 2], Hs[it % 2], cTs[it % 2], sTs[it % 2]
                        it += 1
                        O.tr(pB_.t[:, 0:128], Uc.t[:, gl, :], ident.t[:], R=[Uc.b, ident.b], W=[pB_.b])
                        for m in range(4):
                            O.tr(pA.t[:, m * 128:(m + 1) * 128], Ul[m].t[:, gl, :], ident.t[:], R=[Ul[m].b, ident.b], W=[pA.b])
                        O.act(uf.t[:, 0:NCC], pB_.t[:, 0:NCC], AF.Copy, R=[pB_.b], W=[uf.b])
                        O.act(uf.t[:, NCC:NCH], pA.t[:], AF.Copy, R=[pA.b], W=[uf.b])
                        O.copy(DVE, ur.t[:, 0:NCC][:, ::-1], uf.t[:, 0:NCC], R=[uf.b], W=[ur.b])
                        O.copy(DVE, ur.t[:, NCC:NCH][:, ::-1], uf.t[:, NCC:NCH], R=[uf.b], W=[ur.b])
                        ud = [uf, ur]
                        for ri in range(2):
                            for d in range(2):
                                Wb = Wbm[ri][d]
                                O.mm(SL[ri].t[:], Wb.t[:, g, :], ud[d].t[:, NCC:NCH], d == 0, d == 1,
                                     R=[Wb.b, ud[d].b], W=[SL[ri].b])
                            for d in range(2):
                                Wb = Wbm[ri][d]
                                O.mm(SCb.t[:, ri * NCC:(ri + 1) * NCC], Wb.t[:, g, :], ud[d].t[:, 0:NCC], d == 0, d == 1,
                                     R=[Wb.b, ud[d].b], W=[SCb.b])
                        for ri in range(2):
                            O.act(Ssb.t[:, ri, NCC:NCH], SL[ri].t[:], AF.Copy, R=[SL[ri].b], W=[Ssb.b])
                            O.act(Ssb.t[:, ri, 0:NCC], SCb.t[:, ri * NCC:(ri + 1) * NCC], AF.Copy, R=[SCb.b], W=[Ssb.b])
                        O.dma(sT.t[:], K.TBL.ap()[g, 0], R=[K.bTBL], W=[sT.b])
                        O.dma(cT.t[:], K.TBL.ap()[g, 1], R=[K.bTBL], W=[cT.b])
                        Sre, Sim = Ssb.t[:, 0, :], Ssb.t[:, 1, :]
                        O.tt(DVE, a1.t[:], Sre, cT.t[:], ALU.mult, R=[Ssb.b, cT.b], W=[a1.b])
                        O.tt(POOL, a2.t[:], Sim, sT.t[:], ALU.mult, R=[Ssb.b, sT.b], W=[a2.b])
                        O.tt(DVE, Z.t[:, 0, :], a1.t[:], a2.t[:], ALU.add, R=[a1.b, a2.b], W=[Z.b])
                        O.tt(POOL, a3.t[:], Sim, cT.t[:], ALU.mult, R=[Ssb.b, cT.b], W=[a3.b])
                        O.tt(DVE, a4.t[:], Sre, sT.t[:], ALU.mult, R=[Ssb.b, sT.b], W=[a4.b])
                        O.tt(DVE, Z.t[:, 1, :], a3.t[:], a4.t[:], ALU.subtract, R=[a3.b, a4.b, Z.b], W=[Z.b])
                        rb = r8.t[:, g:g + 1].to_broadcast([128, NCH])
                        for ri in range(2):
                            zz, qq = Z.t[:, ri, :], Q.t[:, ri, :]
                            K.P.op(DVE, (lambda zz, qq, rb: (lambda e: e.tensor_tensor_scan(out=qq, data0=rb, data1=zz, initial=0.0,
                                                                                         op0=ALU.mult, op1=ALU.add)))(zz, qq, rb),
                                   [Z.b, r8.b], [Q.b])
                        Qre, Qim = Q.t[:, 0, :], Q.t[:, 1, :]
                        O.tt(DVE, a1.t[:], Qre, cT.t[:], ALU.mult, R=[Q.b, cT.b], W=[a1.b])
                        O.tt(POOL, a2.t[:], Qim, sT.t[:], ALU.mult, R=[Q.b, sT.b], W=[a2.b])
                        O.tt(DVE, H.t[:, 0, :], a1.t[:], a2.t[:], ALU.subtract, R=[a1.b, a2.b], W=[H.b])
                        O.tt(POOL, a3.t[:], Qim, cT.t[:], ALU.mult, R=[Q.b, cT.b], W=[a3.b])
                        O.tt(DVE, a4.t[:], Qre, sT.t[:], ALU.mult, R=[Q.b, sT.b], W=[a4.b])
                        O.tt(DVE, H.t[:, 1, :], a3.t[:], a4.t[:], ALU.add, R=[a3.b, a4.b, H.b], W=[H.b])
                        for d in range(2):
                            O.mm(PY[d].t[:], M[d].t[:, g, :], ud[d].t[:, NCC:NCH], True, False, R=[M[d].b, ud[d].b], W=[PY[d].b])
                            O.mm(PY[d].t[:], Wcm[d][0].t[:, g, :], H.t[:, 0, NCC - 1:NCH - 1], False, False,
                                 R=[Wcm[d][0].b, H.b], W=[PY[d].b])
                            O.mm(PY[d].t[:], Wcm[d][1].t[:, g, :], H.t[:, 1, NCC - 1:NCH - 1], False, True,
                                 R=[Wcm[d][1].b, H.b], W=[PY[d].b])
                        O.copy(DVE, Yr.t[:, ::-1], PY[1].t[:], R=[PY[1].b], W=[Yr.b])
                        O.tt(DVE, Yg.t[:], PY[0].t[:], Yr.t[:], ALU.add, R=[PY[0].b, Yr.b], W=[Yg.b])
                        for m in range(4):
                            O.tr(TO.t[:, m * 128:(m + 1) * 128], Yg.t[:, m * 128:(m + 1) * 128], ident.t[:], R=[Yg.b, ident.b], W=[TO.b])
                        for m in range(4):
                            O.act(Yt[m].t[:, gl, :], TO.t[:, m * 128:(m + 1) * 128], AF.Copy, R=[TO.b], W=[Yt[m].b])
                    dv = Dbc.t[:, cs].rearrange("p (g c) -> p g c", g=8).unsqueeze(2).to_broadcast([128, 8, 8, 16])
                    for m in range(4):
                        y = Yt[m]
                        go = gout[m % 2]
                        O.tt(DVE, v_gic(g1), v_gic(Ul[m]), dv, ALU.mult, R=[Ul[m].b, Dbc.b], W=[g1.b])
                        O.tt(POOL, y.t[:], y.t[:], g1.t[:], ALU.add, R=[y.b, g1.b], W=[y.b])
                        O.tt(POOL, g1.t[:], y.t[:], y.t[:], ALU.mult, R=[y.b], W=[g1.b])
                        O.ts(DVE, g1.t[:], g1.t[:], 0.044715, ALU.mult, 1.0, ALU.add, R=[g1.b], W=[g1.b])
                        O.tt(POOL, g2.t[:], g1.t[:], y.t[:], ALU.mult, R=[g1.b, y.b], W=[g2.b])
                        O.act(g2.t[:], g2.t[:], AF.Sigmoid, R=[g2.b], W=[g2.b], scale=1.5957691216057308)
                        O.tt(DVE, perm_in(go), v_gic(g2), v_gic(y), ALU.mult, R=[g2.b, y.b], W=[go.b])
                        O.dma(K.YG.ap()[b, 1024 * m:1024 * (m + 1), cs].rearrange("(n i) c -> n i c", i=8), go.t[:], R=[go.b], W=[K.bYG])
            end_phase(K)
    with ExitStack() as es:
        if "c" not in K.s5_parts:
            return
        wg = mk(K, es, "wg", [128, 4, 512], BF16)
        bg = mk(K, es, "bg", [128, 4])
        yt = [mk(K, es, "ytk%d" % i, [128, 512]) for i in range(2)]
        yTf = mk(K, es, "yTf", [128, 4, 512])
        yTb = mk(K, es, "yTb", [128, 4, 512], BF16)
        sg = [mk(K, es, "sg%d" % i, [128, 512]) for i in range(2)]
        SXs = [mk(K, es, "SXs%d" % i, [128, 4, 512], BF16) for i in range(2)]
        ident = K.ident
        O.dma(wg.t[:], K.w_glu.ap().rearrange("(k p) c -> p k c", p=128), W=[wg.b], eng=POOL)
        O.dma(bg.t[:], K.b_glu4.ap(), W=[bg.b])
        n = 0
        for b in range(NB):
            for blk in range(L // 512):
                tok0 = blk * 512
                for t in range(4):
                    y_ = yt[t % 2]
                    O.dma(y_.t[:], K.YG.ap()[b, tok0 + t * 128:tok0 + (t + 1) * 128, :], R=[K.bYG], W=[y_.b])
                    for k in range(4):
                        O.tr(K.ps[k].t[:, t * 128:(t + 1) * 128], y_.t[:, k * 128:(k + 1) * 128], ident.t[:], R=[y_.b, ident.b], W=[K.ps[k].b])
                for k in range(4):
                    O.copy(DVE, yTf.t[:, k, :], K.ps[k].t[:], R=[K.ps[k].b], W=[yTf.b])
                    O.act(yTb.t[:, k, :], yTf.t[:, k, :], AF.Copy, R=[yTf.b], W=[yTb.b])
                sx = SXs[n % 2]
                n += 1
                for co in range(4):
                    pz = K.ps[4 + co % 2]
                    for k in range(4):
                        O.mm(pz.t[:], wg.t[:, k, co * 128:(co + 1) * 128], yTb.t[:, k, :], k == 0, k == 3, R=[wg.b, yTb.b], W=[pz.b])
                    s_ = sg[co % 2]
                    O.act(s_.t[:], pz.t[:], AF.Sigmoid, R=[pz.b, bg.b], W=[s_.b], bias=bg.t[:, co:co + 1])
                    O.tt(DVE, sx.t[:, co, :], yTf.t[:, co, :], s_.t[:], ALU.mult, R=[yTf.b, s_.b], W=[sx.b])
                O.dma(K.SXT.ap()[b].rearrange("(k p) t -> p k t", p=128)[:, :, tok0:tok0 + 512], sx.t[:], R=[sx.b], W=[K.bSXT])
        end_phase(K)


def phase_merge(K):
    nc, O = K.nc, K.O
    with ExitStack() as es:
        wpa = mk(K, es, "wpa", [128, 8, D], BF16)
        wps = mk(K, es, "wps", [128, 4, D], BF16)
        wout = mk(K, es, "wout", [128, 8, D], BF16)
        GT = mk(K, es, "GT2", [128, D])
        OTb = [mk(K, es, "OTb%d" % i, [128, 8, 512], BF16) for i in range(2)]
        SXb = [mk(K, es, "SXb%d" % i, [128, 4, 512], BF16) for i in range(2)]
        GAb = [mk(K, es, "GAb%d" % i, [128, 8, 512]) for i in range(2)]
        GSb = [mk(K, es, "GSb%d" % i, [128, 8, 512]) for i in range(2)]
        mT = mk(K, es, "mT", [128, 8, 512], BF16)
        ta = [mk(K, es, "ta%d" % i, [128, 512]) for i in range(2)]
        tb = [mk(K, es, "tb%d" % i, [128, 512]) for i in range(2)]
        xr = [mk(K, es, "xrm%d" % i, [128, D]) for i in range(2)]
        tmp = mk(K, es, "tmpm", [128, D])
        O.dma(wpa.t[:], K.w_pa.ap().rearrange("(k p) c -> p k c", p=128), W=[wpa.b], eng=POOL)
        O.dma(wps.t[:], K.w_ps.ap().rearrange("(k p) c -> p k c", p=128), W=[wps.b], eng=POOL)
        O.dma(wout.t[:], K.w_out.ap().rearrange("(k p) c -> p k c", p=128), W=[wout.b], eng=POOL)
        bi = 0
        for b in range(NB):
            O.dma(GT.t[:], K.MOD.ap()[b:b + 1, 5 * D:6 * D].partition_broadcast(128), R=[K.bMOD], W=[GT.b])
            for blk in range(L // 512):
                tok0 = blk * 512
                ot, sx, ga, gs = OTb[bi % 2], SXb[bi % 2], GAb[bi % 2], GSb[bi % 2]
                bi += 1
                O.dma(ot.t[:], K.OT.ap()[b].rearrange("(k p) t -> p k t", p=128)[:, :, tok0:tok0 + 512], R=[K.bOT], W=[ot.b])
                O.dma(sx.t[:], K.SXT.ap()[b].rearrange("(k p) t -> p k t", p=128)[:, :, tok0:tok0 + 512], R=[K.bSXT], W=[sx.b])
                O.dma(ga.t[:], K.GA.ap()[b].rearrange("(k p) t -> p k t", p=128)[:, :, tok0:tok0 + 512], R=[K.bG], W=[ga.b])
                O.dma(gs.t[:], K.GS.ap()[b].rearrange("(k p) t -> p k t", p=128)[:, :, tok0:tok0 + 512], R=[K.bG], W=[gs.b])
                for m in range(8):
                    ppa, pps = K.ps[(m % 2) * 2], K.ps[(m % 2) * 2 + 1]
                    for k in range(8):
                        O.mm(ppa.t[:], wpa.t[:, k, m * 128:(m + 1) * 128], ot.t[:, k, :], k == 0, k == 7, R=[wpa.b, ot.b], W=[ppa.b])
                    for k in range(4):
                        O.mm(pps.t[:], wps.t[:, k, m * 128:(m + 1) * 128], sx.t[:, k, :], k == 0, k == 3, R=[wps.b, sx.b], W=[pps.b])
                    a_, b_ = ta[m % 2], tb[m % 2]
                    O.tt(DVE, a_.t[:], ppa.t[:], ga.t[:, m, :], ALU.mult, R=[ppa.b, ga.b], W=[a_.b])
                    O.tt(DVE, b_.t[:], pps.t[:], gs.t[:, m, :], ALU.mult, R=[pps.b, gs.b], W=[b_.b])
                    O.tt(POOL, mT.t[:, m, :], a_.t[:], b_.t[:], ALU.add, R=[a_.b, b_.b], W=[mT.b])
                for t in range(4):
                    xo = xr[t % 2]
                    r0 = tok0 + t * 128
                    O.dma(xo.t[:], K.X1.ap()[b, r0:r0 + 128, :], R=[K.bX1], W=[xo.b])
                    for h in range(2):
                        px = K.ps[4 + h]
                        for k in range(8):
                            O.mm(px.t[:], mT.t[:, k, t * 128:(t + 1) * 128], wout.t[:, k, h * 512:(h + 1) * 512], k == 0, k == 7,
                                 R=[mT.b, wout.b], W=[px.b])
                        O.tt(DVE, tmp.t[:, h * 512:(h + 1) * 512], px.t[:], GT.t[:, h * 512:(h + 1) * 512], ALU.mult,
                             R=[px.b, GT.b], W=[tmp.b])
                    O.tt(POOL, xo.t[:], xo.t[:], tmp.t[:], ALU.add, R=[xo.b, tmp.b], W=[xo.b])
                    O.dma(K.X2.ap()[b, r0:r0 + 128, :], xo.t[:], R=[xo.b], W=[K.bX2])
        end_phase(K)


def build(debug=False, phases=("mod", "ffn1", "inproj", "attn", "s5", "merge", "ffn2"), ext=()):
    nc = bass.Bass("TRN2", target_bir_lowering=False)
    K = Ctx()
    K.nc = nc
    K.debug = debug
    K.s5_parts = "abc"
    K.dbg_names = debug if isinstance(debug, (list, tuple, set)) else ()
    K.s5_limit = None
    for p_ in phases:
        if p_.startswith("s5:"):
            K.s5_parts = p_[3:].replace("1", "")
            K.s5_limit = 1 if "1" in p_ else None
    if any(p_.startswith("s5:") for p_ in phases):
        phases = tuple(phases) + ("s5",)

    def din(name, shape, dtype=F32):
        return nc.dram_tensor(name, list(shape), dtype, kind="ExternalInput")

    def scr(name, shape, dtype=F32):
        if name in ext:
            return nc.dram_tensor(name, list(shape), dtype, kind="ExternalInput")
        return nc.dram_tensor(name, list(shape), dtype, kind="ExternalOutput" if (debug and name in K.dbg_names) else "Internal")

    K.x = din("x", [NB, L, D])
    K.ctx = din("ctx", [NB, LC, D])
    K.cT = din("cT", [128, 8, 3])
    K.w_mod = din("w_mod", [D, 9 * D])
    K.b_mod = din("b_mod", [1, 9 * D])
    K.norm_ffn1 = din("norm_ffn1", [1, D])
    K.w13_ffn1 = din("w13_ffn1", [D, 2 * DFF])
    K.w2_ffn1 = din("w2_ffn1", [DFF, D])
    K.norm_mix = din("norm_mix", [1, D])
    K.w_in = din("w_in", [D, 5632])
    K.q_norm = din("q_norm", [1, 64])
    K.k_norm = din("k_norm", [1, 64])
    K.lamv = din("lamv", [1, 256])
    K.subln = din("subln", [128, 1])
    K.norm_ffn2 = din("norm_ffn2", [1, D])
    K.w13_ffn2 = din("w13_ffn2", [D, 2 * DFF])
    K.w2_ffn2 = din("w2_ffn2", [DFF, D])
    K.w_pa = din("w_pa", [D, D])
    K.w_ps = din("w_ps", [512, D])
    K.w_out = din("w_out", [D, D])
    K.w_glu = din("w_glu", [512, 512])
    K.identd = din("identd", [128, 128])
    K.s5p = din("s5p", [128, 3, 32])
    K.s5B = din("s5B", [128, 2, 32, 16])
    K.s5C = din("s5C", [128, 2, 32, 16])
    K.s5D = din("s5D", [1, 512])
    K.CMd = din("CMd", [128, 2, 128])
    K.nvecd = din("nvecd", [128, LK // 8])
    K.b_glu4 = din("b_glu4", [128, 4])
    K.ropec = din("ropec", [L, 32])
    K.ropes = din("ropes", [L, 32])
    K.out = nc.dram_tensor("out", [NB, L, D], F32, kind="ExternalOutput")

    K.MOD = scr("MOD", [3, 9 * D])
    K.X1 = scr("X1", [NB, L, D])
    K.C1 = scr("C1", [NB, LC, D])
    K.QT = scr("QT", [NB, D, L], BF16)
    K.KT = scr("KT", [NB, D, LK], BF16)
    K.V = scr("V", [NB, LK, D], BF16)
    K.U = scr("U", [NB, LK, 512])
    K.GA = scr("GA", [NB, D, L])
    K.GS = scr("GS", [NB, D, L])
    K.OT = scr("OT", [NB, D, L], BF16)
    K.SXT = scr("SXT", [NB, 512, L], BF16)
    K.X2 = scr("X2", [NB, L, D])
    K.YG = scr("YG", [NB, L, 512])
    K.TBL = scr("TBL", [32, 2, 128, LK // 8])
    K.bTBL = Buf()
    K.bYG = Buf()
    K.bMOD, K.bX1, K.bC1, K.bOUT, K.bIN = Buf(), Buf(), Buf(), Buf(), Buf()
    K.bQT, K.bKT, K.bV, K.bU, K.bG, K.bOT, K.bSXT, K.bX2 = [Buf() for _ in range(8)]

    with ExitStack() as es0:
        K.sems = {e: es0.enter_context(nc.semaphore("s_" + e)) for e in ENGS}
        K.dsems = [es0.enter_context(nc.semaphore("d%d" % i)) for i in range(NDMA + NDMA_P)]
        K.P = Prog(nc)
        K.O = Ops(K.P)
        K.ps = [Tl(es0.enter_context(nc.psum_tensor("ps%d" % i, [128, 512], F32)), "ps%d" % i) for i in range(8)]
        for p_ in K.ps:
            p_.b.excl = True
        K.ident = mk(K, es0, "ident", [128, 128])
        K.O.dma(K.ident.t[:], K.identd.ap(), W=[K.ident.b])
        if "mod" in phases:
            phase_mod(K)
        if "ffn1" in phases:
            phase_ffn(K, "a", K.w13_ffn1, K.w2_ffn1, K.norm_ffn1, 0,
                      ffn_blocks(K.x, K.X1, K.ctx, K.C1), K.bIN, K.bX1)
        if "inproj" in phases:
            phase_inproj(K)
        if "attn" in phases:
            phase_attn(K)
        if "s5" in phases:
            phase_s5(K)
        if "merge" in phases:
            phase_merge(K)
        if "ffn2" in phases:
            phase_ffn(K, "b", K.w13_ffn2, K.w2_ffn2, K.norm_ffn2, 6, ffn_blocks(K.X2, K.out), K.bX2, K.bOUT)
        end_phase(K)
    return nc, K


def host_inputs(inputs):
    f = lambda a: np.ascontiguousarray(np.asarray(a, dtype=np.float32))
    sq = lambda k: f(inputs[k])[0]
    common = {
        "w_mod": sq("w_mod"), "b_mod": sq("b_mod")[None, :],
        "norm_ffn1": sq("norm_ffn1")[None, :], "w13_ffn1": sq("w13_ffn1"), "w2_ffn1": sq("w2_ffn1"),
        "norm_mix": sq("norm_mix")[None, :], "w_in": sq("w_in"),
        "q_norm": sq("q_norm")[None, :], "k_norm": sq("k_norm")[None, :],
        "lamv": np.concatenate([sq("lam_q1"), sq("lam_k1"), sq("lam_q2"), sq("lam_k2")])[None, :],
        "subln": sq("subln")[:, None],
        "norm_ffn2": sq("norm_ffn2")[None, :], "w13_ffn2": sq("w13_ffn2"), "w2_ffn2": sq("w2_ffn2"),
        "w_pa": sq("w_pa"), "w_ps": sq("w_ps"), "w_out": sq("w_out"), "w_glu": sq("w_glu"),
        "b_glu4": np.ascontiguousarray(sq("b_glu").reshape(4, 128).T),
        "identd": np.eye(128, dtype=np.float32),
    }
    lre, lim, ldt = sq("s5_lam_re"), sq("s5_lam_im"), sq("s5_log_dt")
    s5p = np.zeros((128, 3, 32), np.float32)
    s5p[:, 0, :] = lre.transpose(0, 2, 1).reshape(128, 32)
    s5p[:, 1, :] = lim.transpose(0, 2, 1).reshape(128, 32)
    s5p[:, 2, :] = np.repeat(ldt[:, None, :], 64, axis=1).reshape(128, 32)
    common["s5p"] = s5p
    bre, bim = sq("s5_b_re"), sq("s5_b_im")
    common["s5B"] = np.ascontiguousarray(np.stack([bre.transpose(0, 2, 1, 3).reshape(128, 32, 16),
                                                   bim.transpose(0, 2, 1, 3).reshape(128, 32, 16)], axis=1))
    cre, cim = sq("s5_c_re"), sq("s5_c_im")
    common["s5C"] = np.ascontiguousarray(np.stack([cre.transpose(0, 3, 1, 2).reshape(128, 32, 16),
                                                   cim.transpose(0, 3, 1, 2).reshape(128, 32, 16)], axis=1))
    common["s5D"] = sq("s5_d")[None, :]
    jj = np.repeat(np.arange(8), 16)
    cm = np.zeros((128, 2, 128), np.float32)
    cm[:, 0, :] = (jj[None, :] >= jj[:, None])
    cm[:, 1, :] = (jj[:, None] >= jj[None, :])
    common["CMd"] = cm
    common["nvecd"] = np.ascontiguousarray(np.broadcast_to(np.arange(LK // 8, dtype=np.float32), (128, LK // 8)))
    n_freq = 16
    inv = (10000.0 ** (-np.arange(n_freq, dtype=np.float32) / n_freq)).astype(np.float32)
    row = np.repeat(np.arange(64, dtype=np.float32), 64)
    col = np.tile(np.arange(64, dtype=np.float32), 64)
    ang = np.concatenate([row[:, None] * inv, col[:, None] * inv], axis=-1).astype(np.float32)
    common["ropec"] = np.cos(ang).astype(np.float32)
    common["ropes"] = np.sin(ang).astype(np.float32)
    x = f(inputs["x"])
    c = f(inputs["c"])
    ctx = f(inputs["ctx"])
    cc = f(inputs["c_ctx"])
    maps = []
    for i in range(8):
        b0 = NB * i
        cols = np.stack([c[b0], c[b0 + 1], cc], axis=-1)
        cT = np.ascontiguousarray(cols.reshape(8, 128, 3).transpose(1, 0, 2))
        m = dict(common)
        m["x"] = x[b0:b0 + NB]
        m["ctx"] = ctx[b0:b0 + NB]
        m["cT"] = cT
        maps.append(m)
    return maps


def kernel(**inputs):
    nc, K = build()
    maps = host_inputs(inputs)
    res = run_bass_kernel_spmd(nc, maps, core_ids=list(range(8)))
    out = np.concatenate([np.asarray(r["out"]) for r in res.results], axis=0)
    return out.astype(np.float32)
```
